# Optimizing a Trainium2 kernel written in Bass

```python
import jax, jax.numpy as jnp
from jax import lax
import numpy as np

D_MODEL = 1024
BATCH = 4
SEQ = 8192
DEPTH = 2

GRID_W = 64
CTX_LEN = 256
N_MIXERS = 2
EPS = 1e-6

MLA_HEADS = 8
MLA_Q_LORA = 384
MLA_KV_LORA = 256
MLA_NOPE = 128
MLA_ROPE = 64
MLA_V = 128
MLA_QK = MLA_NOPE + MLA_ROPE
ROPE_FREQ = MLA_ROPE // 4
ROPE_BASE = 10000.0
Q_BLOCK = 128

RW_HEAD = 64
RW_HEADS = D_MODEL // RW_HEAD
RW_DECAY_LORA = 64
RW_AAA_LORA = 64
RW_GATE_LORA = 160
RW_GN_EPS = 64e-5

D_FF = ((8 * D_MODEL // 3 + 255) // 256) * 256

kernel_name = 'hybrid_mla_rwkv7_prefix_dit'


def rms_norm(x, g, eps=EPS):
    xf = x.astype(jnp.float32)
    y = xf * lax.rsqrt(jnp.mean(xf * xf, axis=-1, keepdims=True) + eps)
    return (y * g.astype(jnp.float32)).astype(x.dtype)


def modulate(h, shift, scale):
    return h * (1 + scale) + shift


def swiglu(h, w1, w3, w2):
    return (jax.nn.silu(h @ w1) * (h @ w3)) @ w2


def axial_rope_tables(n_tokens):
    rows = n_tokens // GRID_W
    row = jnp.broadcast_to(jnp.arange(rows, dtype=jnp.float32)[:, None], (rows, GRID_W)).reshape(-1)
    col = jnp.broadcast_to(jnp.arange(GRID_W, dtype=jnp.float32)[None, :], (rows, GRID_W)).reshape(-1)
    inv = ROPE_BASE ** (-jnp.arange(ROPE_FREQ, dtype=jnp.float32) / ROPE_FREQ)
    ang = jnp.stack([row[:, None] * inv, col[:, None] * inv], axis=1)
    return jnp.cos(ang), jnp.sin(ang)


def apply_axial_rope(x, cos, sin):
    shp = x.shape
    xf = x.astype(jnp.float32).reshape(shp[:-1] + (2, 2, ROPE_FREQ))
    x1, x2 = xf[..., 0, :], xf[..., 1, :]
    c = cos[None, :, None]
    s = sin[None, :, None]
    out = jnp.stack([x1 * c - x2 * s, x1 * s + x2 * c], axis=-2)
    return out.reshape(shp).astype(x.dtype)


def mla_project(h, w_dqkv, g_q_lora, g_kv_lora, w_uq, w_ukv, g_qn, g_kn):
    B, L, _ = h.shape
    down = h @ w_dqkv
    cq = rms_norm(down[..., :MLA_Q_LORA], g_q_lora)
    ckv = rms_norm(down[..., MLA_Q_LORA:MLA_Q_LORA + MLA_KV_LORA], g_kv_lora)
    k_rope = down[..., MLA_Q_LORA + MLA_KV_LORA:]
    q = (cq @ w_uq).reshape(B, L, MLA_HEADS, MLA_QK)
    kv = (ckv @ w_ukv).reshape(B, L, MLA_HEADS, MLA_NOPE + MLA_V)
    k_nope, v = kv[..., :MLA_NOPE], kv[..., MLA_NOPE:]
    k = jnp.concatenate([k_nope, jnp.broadcast_to(k_rope[:, :, None, :], (B, L, MLA_HEADS, MLA_ROPE))], axis=-1)
    return rms_norm(q, g_qn), rms_norm(k, g_kn), v


def rope_tail(t, cos, sin):
    return jnp.concatenate([t[..., :MLA_NOPE], apply_axial_rope(t[..., MLA_NOPE:], cos, sin)], axis=-1)


def block_attention(q, k, v):
    B, Sq, H, Dq = q.shape
    nb = Sq // Q_BLOCK
    qb = q.reshape(B, nb, Q_BLOCK, H, Dq).transpose(1, 0, 2, 3, 4)
    scale = Dq ** -0.5

    def one(qblk):
        s = jnp.einsum('bqhd,bkhd->bhqk', qblk, k, preferred_element_type=jnp.float32) * scale
        p = jax.nn.softmax(s, axis=-1).astype(v.dtype)
        return jnp.einsum('bhqk,bkhd->bqhd', p, v)

    o = lax.map(one, qb)
    return o.transpose(1, 0, 2, 3, 4).reshape(B, Sq, H * v.shape[-1])


def mla_mixer(h_lat, h_ctx, w_dqkv, g_q_lora, g_kv_lora, w_uq, w_ukv, g_qn, g_kn, w_o, cos, sin, need_ctx_out):
    q_l, k_l, v_l = mla_project(h_lat, w_dqkv, g_q_lora, g_kv_lora, w_uq, w_ukv, g_qn, g_kn)
    q_l = rope_tail(q_l, cos, sin)
    k_l = rope_tail(k_l, cos, sin)
    q_c, k_c, v_c = mla_project(h_ctx, w_dqkv, g_q_lora, g_kv_lora, w_uq, w_ukv, g_qn, g_kn)
    k_all = jnp.concatenate([k_c, k_l], axis=1)
    v_all = jnp.concatenate([v_c, v_l], axis=1)
    o_l = block_attention(q_l, k_all, v_all) @ w_o
    o_c = block_attention(q_c, k_c, v_c) @ w_o if need_ctx_out else None
    return o_l, o_c


def centred_shift_delta(x):
    prev = jnp.pad(x[:, :-1], ((0, 0), (1, 0), (0, 0)))
    nxt = jnp.pad(x[:, 1:], ((0, 0), (0, 1), (0, 0)))
    return 0.5 * (prev + nxt) - x


def rwkv_features(h, need_out, mix, w_r, w_k, w_v, k_k, k_a, dw0, dw1, dw2, ia0, ia1, ia2, g1, g2):
    B, L, D = h.shape
    hd = (B, L, RW_HEADS, RW_HEAD)
    xx = centred_shift_delta(h)
    xr, xw, xk, xv, xa, xg = [h + xx * mix[j] for j in range(6)]
    k = xk @ w_k
    v = (xv @ w_v).reshape(hd)
    kk = (k * k_k).reshape(hd).astype(jnp.float32)
    kk = kk * lax.rsqrt(jnp.sum(kk * kk, axis=-1, keepdims=True) + 1e-12)
    dirs = []
    for d in range(2):
        w_log = -jax.nn.softplus(-(dw0[d] + jnp.tanh(xw @ dw1[d]) @ dw2[d]).astype(jnp.float32)) - 0.5
        decay = jnp.exp(-jnp.exp(w_log)).reshape(hd)
        a = jax.nn.sigmoid((ia0[d] + (xa @ ia1[d]) @ ia2[d]).astype(jnp.float32))
        k_d = (k * (1 + (a - 1) * k_a)).reshape(hd)
        a = a.reshape(hd)
        dirs.append((decay, k_d, -kk, kk * a))
    if need_out:
        r = (xr @ w_r).reshape(hd)
        g = jax.nn.sigmoid(xg @ g1) @ g2
    else:
        r, g = None, None
    return r, v, g, dirs


def wkv7_scan(r, w, k, v, a, b, state0, reverse):
    emit = r is not None
    xs = [w, k, v, a, b] + ([r] if emit else [])
    xs = tuple(jnp.moveaxis(t.astype(jnp.float32), 1, 0) for t in xs)

    def step(S, inp):
        w_t, k_t, v_t, a_t, b_t = inp[:5]
        Sa = jnp.einsum('bhvk,bhk->bhv', S, a_t)
        S = S * w_t[:, :, None, :] + Sa[..., None] * b_t[:, :, None, :] + v_t[..., None] * k_t[:, :, None, :]
        y = jnp.einsum('bhvk,bhk->bhv', S, inp[5]) if emit else None
        return S, y

    S, ys = lax.scan(step, state0, xs, reverse=reverse)
    return S, (jnp.moveaxis(ys, 0, 1) if emit else None)


def rwkv_output(y, r, v, g, dirs, r_k, gn_w, gn_b, w_o):
    B, L, H, N = y.shape
    mu = jnp.mean(y, axis=-1, keepdims=True)
    var = jnp.mean(jnp.square(y - mu), axis=-1, keepdims=True)
    yn = (y - mu) * lax.rsqrt(var + RW_GN_EPS) * gn_w.reshape(H, N) + gn_b.reshape(H, N)
    bonus = sum(jnp.sum(r * dd[1] * r_k, axis=-1, keepdims=True) for dd in dirs) * v
    return ((yn + bonus).reshape(B, L, H * N).astype(g.dtype) * g) @ w_o


def rwkv_mixer(h_lat, h_ctx, mix, w_r, w_k, w_v, w_o, k_k, k_a, r_k, dw0, dw1, dw2, ia0, ia1, ia2, g1, g2, gn_w, gn_b, need_ctx_out):
    fp = (mix, w_r, w_k, w_v, k_k, k_a, dw0, dw1, dw2, ia0, ia1, ia2, g1, g2)
    r_l, v_l, g_l, dirs_l = rwkv_features(h_lat, True, *fp)
    r_c, v_c, g_c, dirs_c = rwkv_features(h_ctx, need_ctx_out, *fp)
    B = h_lat.shape[0]
    y_l, y_c = 0.0, 0.0
    for d, rev in enumerate((False, True)):
        state0 = jnp.zeros((B, RW_HEADS, RW_HEAD, RW_HEAD), jnp.float32)
        dc, kc, ac, bc = dirs_c[d]
        S_c, yc = wkv7_scan(r_c, dc, kc, v_c, ac, bc, state0, rev)
        dl, kl, al, bl = dirs_l[d]
        _, yl = wkv7_scan(r_l, dl, kl, v_l, al, bl, S_c, rev)
        y_l = y_l + yl
        if need_ctx_out:
            y_c = y_c + yc
    o_l = rwkv_output(y_l, r_l, v_l, g_l, dirs_l, r_k, gn_w, gn_b, w_o)
    o_c = rwkv_output(y_c, r_c, v_c, g_c, dirs_c, r_k, gn_w, gn_b, w_o) if need_ctx_out else None
    return o_l, o_c


def setup_inputs(seed: int = 0) -> dict:
    key = jax.random.key(seed)
    ks = iter(jax.random.split(key, 48))
    f32 = jnp.float32

    def nrm(shape, scale):
        return jax.random.normal(next(ks), shape, f32) * scale

    def gain(shape):
        return 1.0 + 0.05 * jax.random.normal(next(ks), shape, f32)

    def unif(shape, lo, hi):
        return jax.random.uniform(next(ks), shape, f32, lo, hi)

    D = D_MODEL
    n_a = (DEPTH + N_MIXERS - 1) // N_MIXERS
    n_b = DEPTH // N_MIXERS
    return {
        'x': nrm((BATCH, SEQ, D), 1.0),
        'c': nrm((BATCH, D), 1.0),
        'ctx': nrm((BATCH, CTX_LEN, D), 1.0),
        'c_ctx': nrm((D,), 1.0),
        'ada_w': nrm((DEPTH, D, 6 * D), 0.5 * D ** -0.5),
        'ada_b': nrm((DEPTH, 6 * D), 0.02),
        'norm_mix': gain((DEPTH, D)),
        'norm_ffn': gain((DEPTH, D)),
        'ffn_w1': nrm((DEPTH, D, D_FF), D ** -0.5),
        'ffn_w3': nrm((DEPTH, D, D_FF), D ** -0.5),
        'ffn_w2': nrm((DEPTH, D_FF, D), D_FF ** -0.5),
        'mla_w_dqkv': nrm((n_a, D, MLA_Q_LORA + MLA_KV_LORA + MLA_ROPE), D ** -0.5),
        'mla_g_q_lora': gain((n_a, MLA_Q_LORA)),
        'mla_g_kv_lora': gain((n_a, MLA_KV_LORA)),
        'mla_w_uq': nrm((n_a, MLA_Q_LORA, MLA_HEADS * MLA_QK), MLA_Q_LORA ** -0.5),
        'mla_w_ukv': nrm((n_a, MLA_KV_LORA, MLA_HEADS * (MLA_NOPE + MLA_V)), MLA_KV_LORA ** -0.5),
        'mla_g_qn': gain((n_a, MLA_QK)),
        'mla_g_kn': gain((n_a, MLA_QK)),
        'mla_w_o': nrm((n_a, MLA_HEADS * MLA_V, D), (MLA_HEADS * MLA_V) ** -0.5),
        'rw_mix': unif((n_b, 6, D), 0.0, 1.0),
        'rw_w_r': nrm((n_b, D, D), D ** -0.5),
        'rw_w_k': nrm((n_b, D, D), D ** -0.5),
        'rw_w_v': nrm((n_b, D, D), D ** -0.5),
        'rw_w_o': nrm((n_b, D, D), D ** -0.5),
        'rw_k_k': 0.85 + nrm((n_b, D), 0.05),
        'rw_k_a': gain((n_b, D)),
        'rw_r_k': nrm((n_b, RW_HEADS, RW_HEAD), 0.1),
        'rw_decay_w0': unif((n_b, 2, D), -6.0, 0.0),
        'rw_decay_w1': nrm((n_b, 2, D, RW_DECAY_LORA), D ** -0.5),
        'rw_decay_w2': nrm((n_b, 2, RW_DECAY_LORA, D), 0.5 * RW_DECAY_LORA ** -0.5),
        'rw_iclr_a0': nrm((n_b, 2, D), 0.5),
        'rw_iclr_a1': nrm((n_b, 2, D, RW_AAA_LORA), D ** -0.5),
        'rw_iclr_a2': nrm((n_b, 2, RW_AAA_LORA, D), 0.5 * RW_AAA_LORA ** -0.5),
        'rw_gate_g1': nrm((n_b, D, RW_GATE_LORA), D ** -0.5),
        'rw_gate_g2': nrm((n_b, RW_GATE_LORA, D), RW_GATE_LORA ** -0.5),
        'rw_gn_w': gain((n_b, D)),
        'rw_gn_b': nrm((n_b, D), 0.02),
    }


def reference(x, c, ctx, c_ctx, ada_w, ada_b, norm_mix, norm_ffn, ffn_w1, ffn_w3, ffn_w2,
              mla_w_dqkv, mla_g_q_lora, mla_g_kv_lora, mla_w_uq, mla_w_ukv, mla_g_qn, mla_g_kn, mla_w_o,
              rw_mix, rw_w_r, rw_w_k, rw_w_v, rw_w_o, rw_k_k, rw_k_a, rw_r_k,
              rw_decay_w0, rw_decay_w1, rw_decay_w2, rw_iclr_a0, rw_iclr_a1, rw_iclr_a2,
              rw_gate_g1, rw_gate_g2, rw_gn_w, rw_gn_b):
    n_lat = x.shape[1]
    cos, sin = axial_rope_tables(n_lat)
    for i in range(DEPTH):
        last = i == DEPTH - 1
        j = i // N_MIXERS
        mod_l = (jax.nn.silu(c) @ ada_w[i] + ada_b[i])[:, None, :]
        mod_c = jax.nn.silu(c_ctx) @ ada_w[i] + ada_b[i]
        sh_m, sc_m, ga_m, sh_f, sc_f, ga_f = jnp.split(mod_l, 6, axis=-1)
        csh_m, csc_m, cga_m, csh_f, csc_f, cga_f = jnp.split(mod_c, 6, axis=-1)
        h_l = modulate(rms_norm(x, norm_mix[i]), sh_m, sc_m)
        h_c = modulate(rms_norm(ctx, norm_mix[i]), csh_m, csc_m)
        if i % N_MIXERS == 0:
            o_l, o_c = mla_mixer(h_l, h_c, mla_w_dqkv[j], mla_g_q_lora[j], mla_g_kv_lora[j], mla_w_uq[j],
                                 mla_w_ukv[j], mla_g_qn[j], mla_g_kn[j], mla_w_o[j], cos, sin, not last)
        else:
            o_l, o_c = rwkv_mixer(h_l, h_c, rw_mix[j], rw_w_r[j], rw_w_k[j], rw_w_v[j], rw_w_o[j], rw_k_k[j],
                                  rw_k_a[j], rw_r_k[j], rw_decay_w0[j], rw_decay_w1[j], rw_decay_w2[j],
                                  rw_iclr_a0[j], rw_iclr_a1[j], rw_iclr_a2[j], rw_gate_g1[j], rw_gate_g2[j],
                                  rw_gn_w[j], rw_gn_b[j], not last)
        x = x + ga_m * o_l
        x = x + ga_f * swiglu(modulate(rms_norm(x, norm_ffn[i]), sh_f, sc_f), ffn_w1[i], ffn_w3[i], ffn_w2[i])
        if not last:
            ctx = ctx + cga_m * o_c
            ctx = ctx + cga_f * swiglu(modulate(rms_norm(ctx, norm_ffn[i]), csh_f, csc_f), ffn_w1[i], ffn_w3[i], ffn_w2[i])
    return x
```

```python
from contextlib import ExitStack
import math
import numpy as np
import ml_dtypes
import concourse.bass as bass
import concourse.mybir as mybir
from concourse.bass_utils import run_bass_kernel_spmd

F32 = mybir.dt.float32
BF16 = mybir.dt.bfloat16
AF = mybir.ActivationFunctionType
ALU = mybir.AluOpType
AX = mybir.AxisListType

D = 1024
KC = 8
SEQ = 8192
CTX = 256
NCORE = 8
EPS = 1e-6
DFF = 2816
HB = DFF // 128
NTILE = (SEQ + CTX) // 128
NOWN = (CTX + SEQ // 2) // 128
NQ = NOWN * 128

NDMASEM = 6
DBG_NT = None
DBG_PART = 99


class Res:
    __slots__ = ("name", "w", "r")

    def __init__(self, name):
        self.name = name
        self.w = None
        self.r = {}


class Sched:
    def __init__(self, nc, stack):
        self.nc = nc
        self.eng = {"pe": nc.tensor, "act": nc.scalar, "dve": nc.vector,
                    "pool": nc.gpsimd, "sp": nc.sync}
        self.sem = {k: stack.enter_context(nc.semaphore("s_" + k)) for k in self.eng}
        self.cnt = {k: 0 for k in self.eng}
        self.seen = {k: {} for k in self.eng}
        self.dq = {}
        for q in ("sp", "pool", "act"):
            sems = [stack.enter_context(nc.semaphore("d_%s%d" % (q, i))) for i in range(NDMASEM)]
            self.dq[q] = {"sems": sems, "n": 0}
        self.semobj = dict(("e_" + k, v) for k, v in self.sem.items())
        for q, d in self.dq.items():
            for i, s in enumerate(d["sems"]):
                self.semobj["d_%s_%d" % (q, i)] = s
        self.nres = 0
        self.ninst = 0

    def res(self, name=None):
        self.nres += 1
        return Res(name or "r%d" % self.nres)

    def _need(self, e, dep, waits):
        if dep is None:
            return
        key, val, owner = dep
        if owner == e and e == "pe":
            return
        if self.seen[e].get(key, 0) >= val:
            return
        waits[key] = max(waits.get(key, 0), val)

    def _collect(self, e, reads, writes, waits):
        for r in reads:
            self._need(e, r.w, waits)
        for w in writes:
            self._need(e, w.w, waits)
            for d in w.r.values():
                self._need(e, d, waits)

    def _emit_waits(self, e, waits):
        for key, val in waits.items():
            self.eng[e].wait_ge(self.semobj[key], val)
            self.seen[e][key] = val
            self.ninst += 1

    def _mark(self, dep, reads, writes):
        for r in reads:
            o = r.r.get(dep[0])
            if o is None or o[1] < dep[1]:
                r.r[dep[0]] = dep
        for w in writes:
            w.w = dep
            w.r = {}

    def op(self, e, fn, reads=(), writes=()):
        waits = {}
        self._collect(e, reads, writes, waits)
        self._emit_waits(e, waits)
        ins = fn(self.eng[e])
        self.cnt[e] += 1
        self.ninst += 1
        ins.then_inc(self.sem[e], 1)
        self._mark(("e_" + e, self.cnt[e], e), reads, writes)
        return ins

    def dma(self, q, out, in_, reads=(), writes=(), **kw):
        d = self.dq[q]
        i = d["n"] % NDMASEM
        rnd = d["n"] // NDMASEM
        key = "d_%s_%d" % (q, i)
        waits = {}
        if rnd > 0 and self.seen[q].get(key, 0) < 16 * rnd:
            waits[key] = 16 * rnd
        self._collect(q, reads, writes, waits)
        self._emit_waits(q, waits)
        ins = self.eng[q].dma_start(out=out, in_=in_, **kw)
        ins.then_inc(d["sems"][i], 16)
        d["n"] += 1
        self.ninst += 1
        self._mark((key, 16 * (rnd + 1), "dma_" + q), reads, writes)
        return ins

    def barrier(self):
        tgt = {}
        for k, v in self.cnt.items():
            if v:
                tgt["e_" + k] = (v, k)
        for q, d in self.dq.items():
            for i in range(NDMASEM):
                n = (d["n"] - 1 - i) // NDMASEM + 1 if d["n"] > i else 0
                if n:
                    tgt["d_%s_%d" % (q, i)] = (16 * n, "dma_" + q)
        for e in self.eng:
            waits = {}
            for key, (val, owner) in tgt.items():
                if owner == e:
                    continue
                if self.seen[e].get(key, 0) < val:
                    waits[key] = val
            self._emit_waits(e, waits)

    def finish(self, outs):
        waits = {}
        for r in outs:
            self._need("sp", r.w, waits)
        self._emit_waits("sp", waits)


class Ring:
    def __init__(self, items):
        self.items = items
        self.i = 0

    def next(self):
        it = self.items[self.i % len(self.items)]
        self.i += 1
        return it


class Ctx:
    def __init__(self, nc, stack):
        self.nc = nc
        self.S = Sched(nc, stack)
        self.n = 0

    def sb(self, st, shape, dt=F32):
        self.n += 1
        t = st.enter_context(self.nc.sbuf_tensor("sb%d" % self.n, shape, dt))
        return t, self.S.res()

    def ps(self, st, shape, dt=F32):
        self.n += 1
        t = st.enter_context(self.nc.psum_tensor("ps%d" % self.n, shape, dt))
        return t, self.S.res()

    def ring(self, st, n, shape, dt=F32, psum=False):
        return Ring([(self.ps if psum else self.sb)(st, shape, dt) for _ in range(n)])

    def dram(self, name, shape, dt=F32):
        return self.nc.dram_tensor(name, shape, dt).ap(), self.S.res()


def _inp(nc, name, shape, dt=F32):
    return nc.dram_tensor(name, shape, dt, kind="ExternalInput").ap()


def _out(nc, name, shape, dt=F32):
    return nc.dram_tensor(name, shape, dt, kind="ExternalOutput").ap()


class Common:
    def __init__(self, C, G, nc, layer_tag):
        self.C, self.G, self.nc = C, G, nc
        S = C.S
        self.ident_in = _inp(nc, "ident", [128, 128])
        self.identf, self.idres = C.sb(G, [128, 128])
        self.identb, _ = C.sb(G, [128, 128], BF16)
        self.onesf, _ = C.sb(G, [128, 128])
        self.onesb, _ = C.sb(G, [128, 128], BF16)
        S.dma("sp", self.identf[:], self.ident_in, writes=[self.idres])
        S.op("dve", lambda e: e.tensor_copy(self.identb[:], self.identf[:]), reads=[self.idres], writes=[self.idres])
        S.op("dve", lambda e: e.memset(self.onesf[:], 1.0), writes=[self.idres])
        S.op("dve", lambda e: e.memset(self.onesb[:], 1.0), writes=[self.idres])

    def modulation(self):
        C, G, nc, S = self.C, self.G, self.nc, self.C.S
        cT = _inp(nc, "cT", [128, KC, 2])
        ada_w = _inp(nc, "ada_w", [D, 6 * D])
        ada_bT = _inp(nc, "ada_bT", [128, 48])
        gmix = _inp(nc, "gmix", [128, KC])
        gffn = _inp(nc, "gffn", [128, KC])
        self.modcol, self.modres = C.sb(G, [128, 2, 48])
        self.Am, _ = C.sb(G, [128, 2, KC])
        self.Af, _ = C.sb(G, [128, 2, KC])
        self.gabc = [[C.sb(G, [128, D]) for _ in range(2)] for _ in range(2)]
        mr = self.modres
        with ExitStack() as st:
            sc, scr = C.sb(st, [128, KC, 2])
            bT, _ = C.sb(st, [128, 48])
            gm, _ = C.sb(st, [128, KC])
            gf, _ = C.sb(st, [128, KC])
            stage = C.ring(st, 2, [128, 6 * D])
            dgr = C.ring(st, 2, [128, 128])
            psm, psmr = C.ps(st, [128, 96])
            psb = C.ring(st, 2, [128, 512], psum=True)
            S.dma("sp", sc[:], cT, writes=[scr])
            S.dma("sp", bT[:], ada_bT, writes=[scr])
            S.dma("sp", gm[:], gmix, writes=[scr])
            S.dma("sp", gf[:], gffn, writes=[scr])
            S.op("act", lambda e: e.activation(sc[:], sc[:], AF.Silu), reads=[scr], writes=[scr])
            for kc in range(KC):
                stg, sr = stage.next()
                S.dma("sp" if kc % 2 == 0 else "pool", stg[:], ada_w[kc * 128:(kc + 1) * 128, :], writes=[sr])
                for oc in range(48):
                    S.op("pe", lambda e: e.matmul(psm[:, oc * 2:oc * 2 + 2], stg[:, oc * 128:(oc + 1) * 128],
                                                  sc[:, kc, :], start=(kc == 0 and oc == 0), stop=(kc == KC - 1),
                                                  skip_group_check=True),
                         reads=[sr, scr], writes=[psmr])
            pv = psm[:, :].rearrange("p (o j) -> p o j", j=2)
            for j in range(2):
                S.op("dve", lambda e: e.tensor_tensor(self.modcol[:, j, :], pv[:, :, j], bT[:], ALU.add),
                     reads=[psmr, scr], writes=[mr])
            for j in range(2):
                S.op("dve", lambda e: e.scalar_tensor_tensor(self.Am[:, j, :], self.modcol[:, j, 8:16], 1.0, gm[:],
                                                             ALU.add, ALU.mult), reads=[mr, scr], writes=[mr])
                S.op("dve", lambda e: e.scalar_tensor_tensor(self.Af[:, j, :], self.modcol[:, j, 32:40], 1.0, gf[:],
                                                             ALU.add, ALU.mult), reads=[mr, scr], writes=[mr])
            for mf, which in ((0, 2), (1, 5)):
                for j in range(2):
                    gt, gr = self.gabc[mf][j]
                    for half in range(2):
                        pb, pbr = psb.next()
                        for cc in range(4):
                            c = half * 4 + cc
                            dg, dgres = dgr.next()
                            S.op("dve", lambda e: e.tensor_scalar(dg[:], self.identf[:],
                                                                  self.modcol[:, j, which * 8 + c:which * 8 + c + 1],
                                                                  None, ALU.mult), reads=[mr, self.idres], writes=[dgres])
                            S.op("pe", lambda e: e.matmul(pb[:, cc * 128:(cc + 1) * 128], self.onesf[:], dg[:],
                                                          start=True, stop=True), reads=[dgres, self.idres], writes=[pbr])
                        S.op("act", lambda e: e.activation(gt[:, half * 512:(half + 1) * 512], pb[:], AF.Copy),
                             reads=[pbr], writes=[gr])
            S.barrier()

    def norm_tmp(self, st, npst=2):
        C = self.C
        return dict(junk=C.sb(st, [128, D], BF16), ss=C.ring(st, 2, [128, 4]), xn=C.ring(st, 2, [128, D], BF16),
                    pst=C.ring(st, npst, [128, KC, 128], BF16, psum=True))

    def norm_T(self, tmp, x, xres, A, B, hT, hTres, col0):
        S = self.C.S
        junk, jres = tmp["junk"]
        ss, ssr = tmp["ss"].next()
        xn, xnr = tmp["xn"].next()
        pst, pstr = tmp["pst"].next()
        mr = self.modres
        S.op("dve", lambda e: e.memset(ss[:], 0.0), writes=[ssr])
        S.op("act", lambda e: e.activation(junk[:], x, AF.Square, accum_out=ss[:, 0:1]),
             reads=[xres], writes=[jres, ssr])
        S.op("act", lambda e: e.activation(ss[:, 1:2], ss[:, 0:1], AF.Sqrt, scale=1.0 / D, bias=EPS),
             reads=[ssr], writes=[ssr])
        S.op("dve", lambda e: e.reciprocal(ss[:, 1:2], ss[:, 1:2]), reads=[ssr], writes=[ssr])
        S.op("act", lambda e: e.activation(xn[:], x, AF.Copy, scale=ss[:, 1:2]), reads=[xres, ssr], writes=[xnr])
        for c in range(KC):
            S.op("pe", lambda e: e.transpose(pst[:, c, :], xn[:, c * 128:(c + 1) * 128], self.identb[:]),
                 reads=[xnr, self.idres], writes=[pstr])
        for c in range(KC):
            if c % 2 == 0:
                S.op("act", lambda e: e.activation(hT[:, c, col0:col0 + 128], pst[:, c, :], AF.Identity,
                                                   scale=A[:, c:c + 1], bias=B[:, c:c + 1]),
                     reads=[pstr, mr], writes=[hTres])
            else:
                S.op("dve", lambda e: e.tensor_scalar(hT[:, c, col0:col0 + 128], pst[:, c, :], A[:, c:c + 1],
                                                      B[:, c:c + 1], ALU.mult, ALU.add),
                     reads=[pstr, mr], writes=[hTres])

    def ffn_phase(self, tiles, xin, xinres, xtmp, xtmpres, xout, xoutres):
        C, nc, S = self.C, self.nc, self.C.S
        w1 = _inp(nc, "w1", [D, DFF])
        w3 = _inp(nc, "w3", [D, DFF])
        w2 = _inp(nc, "w2", [DFF, D])
        HH = HB // 2
        FH = DFF // 2
        for half in range(2):
            acc, accres = (xin, xinres) if half == 0 else (xtmp, xtmpres)
            dst, dstres = (xtmp, xtmpres) if half == 0 else (xout, xoutres)
            with ExitStack() as st:
                w1b, w1r = C.sb(st, [128, KC, FH], BF16)
                w3b, w3r = C.sb(st, [128, KC, FH], BF16)
                w2b, w2r = C.sb(st, [128, HH, D], BF16)
                stage = C.ring(st, 2, [128, FH])
                k = 0
                for (w, wb, wr) in ((w1, w1b, w1r), (w3, w3b, w3r)):
                    for c in range(KC):
                        stg, sr = stage.next()
                        S.dma("sp" if k % 2 == 0 else "pool", stg[:], w[c * 128:(c + 1) * 128, half * FH:(half + 1) * FH],
                              writes=[sr])
                        S.op("pool" if k % 2 == 0 else "dve", lambda e: e.tensor_copy(wb[:, c, :], stg[:]),
                             reads=[sr], writes=[wr])
                        k += 1
                for c in range(HH):
                    stg, sr = stage.next()
                    r0 = (half * HH + c) * 128
                    S.dma("sp" if k % 2 == 0 else "pool", stg[:, 0:D], w2[r0:r0 + 128, :], writes=[sr])
                    S.op("pool" if k % 2 == 0 else "dve", lambda e: e.tensor_copy(w2b[:, c, :], stg[:, 0:D]),
                         reads=[sr], writes=[w2r])
                    k += 1
                tmp = self.norm_tmp(st)
                xr = C.ring(st, 2, [128, D])
                hTr = C.ring(st, 2, [128, KC, 512], BF16)
                aTr = C.ring(st, 1, [128, HH, 512], BF16)
                sgr = C.ring(st, 2, [128, 512])
                psg = C.ring(st, 2, [128, 512], psum=True)
                psu = C.ring(st, 2, [128, 512], psum=True)
                pso = C.ring(st, 2, [128, 512], psum=True)
                tr = C.ring(st, 2, [128, D])
                for b0 in range(0, len(tiles), 4):
                    blk = tiles[b0:b0 + 4]
                    n = len(blk) * 128
                    hT, hTres = hTr.next()
                    for t, (row0, j) in enumerate(blk):
                        x, xres = xr.next()
                        S.dma("sp", x[:], xin[row0:row0 + 128, :], reads=[xinres], writes=[xres])
                        self.norm_T(tmp, x[:], xres, self.Af[:, j, :], self.modcol[:, j, 24:32], hT, hTres, t * 128)
                    aT, aTres = aTr.next()
                    for hb in range(HH):
                        pg, pgr = psg.next()
                        pu, pur = psu.next()
                        for c in range(KC):
                            S.op("pe", lambda e: e.matmul(pg[:, 0:n], w1b[:, c, hb * 128:(hb + 1) * 128], hT[:, c, 0:n],
                                                          start=(c == 0), stop=(c == KC - 1)),
                                 reads=[w1r, hTres], writes=[pgr])
                        for c in range(KC):
                            S.op("pe", lambda e: e.matmul(pu[:, 0:n], w3b[:, c, hb * 128:(hb + 1) * 128], hT[:, c, 0:n],
                                                          start=(c == 0), stop=(c == KC - 1)),
                                 reads=[w3r, hTres], writes=[pur])
                        sg, sgres = sgr.next()
                        S.op("act", lambda e: e.activation(sg[:, 0:n], pg[:, 0:n], AF.Silu), reads=[pgr], writes=[sgres])
                        S.op("dve", lambda e: e.tensor_tensor(aT[:, hb, 0:n], sg[:, 0:n], pu[:, 0:n], ALU.mult),
                             reads=[sgres, pur], writes=[aTres])
                    for t, (row0, j) in enumerate(blk):
                        x, xres = xr.next()
                        S.dma("sp", x[:], acc[row0:row0 + 128, :], reads=[accres], writes=[xres])
                        gt, gr = self.gabc[1][j]
                        tt, ttr = tr.next()
                        for nn in range(2):
                            po, por = pso.next()
                            for hb in range(HH):
                                S.op("pe", lambda e: e.matmul(po[:], aT[:, hb, t * 128:(t + 1) * 128],
                                                              w2b[:, hb, nn * 512:(nn + 1) * 512],
                                                              start=(hb == 0), stop=(hb == HH - 1)),
                                     reads=[aTres, w2r], writes=[por])
                            S.op("dve", lambda e: e.tensor_tensor(tt[:, nn * 512:(nn + 1) * 512], po[:],
                                                                  gt[:, nn * 512:(nn + 1) * 512], ALU.mult),
                                 reads=[por, gr], writes=[ttr])
                        S.op("pool", lambda e: e.tensor_tensor(tt[:], tt[:], x[:], ALU.add), reads=[ttr, xres], writes=[ttr])
                        S.dma("pool", dst[row0:row0 + 128, :], tt[:], reads=[ttr], writes=[dstres])
                S.barrier()


HEADS = 8
QK = 192
NOPE = 128
ROPE = 64


class MLA:
    def __init__(self, cm):
        self.cm = cm
        self.C, self.G, self.nc = cm.C, cm.G, cm.nc
        C, G, nc = self.C, self.G, self.nc
        self.xf = _inp(nc, "xf", [SEQ, D])
        self.ctxl = _inp(nc, "ctxl", [CTX, D])
        self.xres = C.S.res()
        self.KTd, self.KTres = C.dram("KTd", [HEADS, 128, NTILE * 128], BF16)
        self.Vd, self.Vres = C.dram("Vd", [NTILE, 128, HEADS * 128], BF16)
        self.QNd, self.QNres = C.dram("QNd", [HEADS, 128, NQ], BF16)
        self.QRd, self.QRres = C.dram("QRd", [HEADS, ROPE, NQ], BF16)
        self.ATd, self.ATres = C.dram("ATd", [HEADS, 128, NQ], BF16)
        self.KRT, self.KRTres = C.sb(G, [ROPE, NTILE * 128], BF16)
        self.rstdk, self.rstdkres = C.sb(G, [128, NTILE, HEADS])

    def src_rows(self, ti):
        if ti < 2:
            return self.ctxl[ti * 128:(ti + 1) * 128, :], 1
        return self.xf[(ti - 2) * 128:(ti - 1) * 128, :], 0

    def token_pass(self):
        C, nc, S, cm = self.C, self.nc, self.C.S, self.cm
        wdq = _inp(nc, "wdq", [D, 704])
        gql = _inp(nc, "gql", [128, 3])
        gkvl = _inp(nc, "gkvl", [128, 2])
        wuq = _inp(nc, "wuq", [384, 1536])
        wukv = _inp(nc, "wukv", [256, 2048])
        gqn_in = _inp(nc, "gqn", [128, QK])
        gkn_in = _inp(nc, "gkn", [128, QK])
        cos_in = _inp(nc, "cosT", [128, NTILE, 32])
        sin_in = _inp(nc, "sinT", [128, NTILE, 32])
        scale = QK ** -0.5
        with ExitStack() as st:
            wdqb, wr = C.sb(st, [128, KC, 704], BF16)
            wuqb, _ = C.sb(st, [128, 3, 1536], BF16)
            wukvb, _ = C.sb(st, [128, 2, 2048], BF16)
            gq, cr = C.sb(st, [128, 3])
            gkv, _ = C.sb(st, [128, 2])
            gqn, _ = C.sb(st, [128, QK])
            gkn, _ = C.sb(st, [128, QK])
            gq2, _ = C.sb(st, [128, QK])
            cosT, _ = C.sb(st, [128, NTILE, 32])
            sinT, _ = C.sb(st, [128, NTILE, 32])
            for t_, a_ in ((gq, gql), (gkv, gkvl), (gqn, gqn_in), (gkn, gkn_in), (cosT, cos_in), (sinT, sin_in)):
                S.dma("sp", t_[:], a_, writes=[cr])
            S.op("dve", lambda e: e.scalar_tensor_tensor(gq2[:, 0:NOPE], gqn[:, 0:NOPE], scale, gkn[:, 0:NOPE],
                                                         ALU.mult, ALU.mult), reads=[cr], writes=[cr])
            S.op("dve", lambda e: e.tensor_scalar(gq2[:, NOPE:QK], gqn[:, NOPE:QK], scale, None, ALU.mult),
                 reads=[cr], writes=[cr])
            with ExitStack() as st2:
                stage = C.ring(st2, 2, [128, 5632])
                stg, sr = stage.next()
                sv = stg[:, 0:KC * 704].rearrange("p (c n) -> p c n", c=KC)
                S.dma("sp", sv, wdq.rearrange("(c p) n -> p c n", p=128), writes=[sr])
                S.op("pool", lambda e: e.tensor_copy(wdqb[:], sv), reads=[sr], writes=[wr])
                stg, sr = stage.next()
                sv2 = stg[:, 0:3 * 1536].rearrange("p (c n) -> p c n", c=3)
                S.dma("pool", sv2, wuq.rearrange("(c p) n -> p c n", p=128), writes=[sr])
                for c in range(3):
                    S.op("dve", lambda e: e.tensor_scalar(wuqb[:, c, :], sv2[:, c, :], gq[:, c:c + 1], None, ALU.mult),
                         reads=[sr, cr], writes=[wr])
                stg, sr = stage.next()
                sv3 = stg[:, 0:2 * 2048].rearrange("p (c n) -> p c n", c=2)
                S.dma("sp", sv3, wukv.rearrange("(c p) n -> p c n", p=128), writes=[sr])
                for c in range(2):
                    S.op("dve", lambda e: e.tensor_scalar(wukvb[:, c, :], sv3[:, c, :], gkv[:, c:c + 1], None, ALU.mult),
                         reads=[sr, cr], writes=[wr])
                S.barrier()
            tmp = cm.norm_tmp(st, 1)
            xr = C.ring(st, 2, [128, D])
            hTr = C.ring(st, 2, [128, KC, 128], BF16)
            dnr = C.ring(st, 2, [128, 704])
            ssr = C.ring(st, 2, [128, 32])
            cbr = C.ring(st, 2, [128, 640], BF16)
            cTr = C.ring(st, 2, [128, 5, 128], BF16)
            vbr = C.ring(st, 2, [128, HEADS, 128], BF16)
            kTbr = C.ring(st, 2, [128, HEADS, 128], BF16)
            krr = C.ring(st, 2, [128, 6, ROPE])
            krbr = C.ring(st, 2, [128, ROPE], BF16)
            qsr = C.ring(st, 1, [128, 1536])
            qnr = C.ring(st, 1, [128, HEADS, QK])
            qtr = C.ring(st, 1, [128, 4, HEADS, 32])
            qbr = C.ring(st, 2, [128, HEADS, QK], BF16)
            qTnr = C.ring(st, 2, [128, HEADS, 128], BF16)
            qTrr = C.ring(st, 2, [ROPE, HEADS, 128], BF16)
            junk2, j2r = C.sb(st, [128, 1536], BF16)
            psA = C.ring(st, 1, [128, 512], psum=True)
            psB = C.ring(st, 1, [128, 512], psum=True)
            psW = C.ring(st, 4, [128, 512], psum=True)
            pT = C.ring(st, 1, [128, HEADS, 128], BF16, psum=True)
            for ti in range(NTILE if DBG_NT is None else DBG_NT):
                own = ti < NOWN
                src, j = self.src_rows(ti)
                x, xres = xr.next()
                S.dma("sp", x[:], src, writes=[xres])
                hT, hTres = hTr.next()
                cm.norm_T(tmp, x[:], xres, cm.Am[:, j, :], cm.modcol[:, j, 0:8], hT, hTres, 0)
                if DBG_PART < 1:
                    continue
                pa, par = psA.next()
                pb, pbr = psB.next()
                for c in range(KC):
                    S.op("pe", lambda e: e.matmul(pa[:], hT[:, c, :], wdqb[:, c, 0:512], start=(c == 0), stop=(c == KC - 1)),
                         reads=[hTres, wr], writes=[par])
                for c in range(KC):
                    S.op("pe", lambda e: e.matmul(pb[:, 0:192], hT[:, c, :], wdqb[:, c, 512:704], start=(c == 0),
                                                  stop=(c == KC - 1)), reads=[hTres, wr], writes=[pbr])
                dn, dnres = dnr.next()
                S.op("act", lambda e: e.activation(dn[:, 0:512], pa[:], AF.Copy), reads=[par], writes=[dnres])
                S.op("dve", lambda e: e.tensor_copy(dn[:, 512:704], pb[:, 0:192]), reads=[pbr], writes=[dnres])
                ss, ssres = ssr.next()
                S.op("dve", lambda e: e.memset(ss[:], 0.0), writes=[ssres])
                for k_, (a0, a1) in enumerate(((0, 384), (384, 640), (640, 704))):
                    S.op("act", lambda e: e.activation(junk2[:, 0:a1 - a0], dn[:, a0:a1], AF.Square,
                                                       accum_out=ss[:, k_:k_ + 1]), reads=[dnres], writes=[j2r, ssres])
                S.op("act", lambda e: e.activation(ss[:, 4:5], ss[:, 0:1], AF.Sqrt, scale=1.0 / 384, bias=EPS),
                     reads=[ssres], writes=[ssres])
                S.op("act", lambda e: e.activation(ss[:, 5:6], ss[:, 1:2], AF.Sqrt, scale=1.0 / 256, bias=EPS),
                     reads=[ssres], writes=[ssres])
                S.op("dve", lambda e: e.reciprocal(ss[:, 4:6], ss[:, 4:6]), reads=[ssres], writes=[ssres])
                cb, cbres = cbr.next()
                S.op("act", lambda e: e.activation(cb[:, 0:384], dn[:, 0:384], AF.Copy, scale=ss[:, 4:5]),
                     reads=[dnres, ssres], writes=[cbres])
                S.op("act", lambda e: e.activation(cb[:, 384:640], dn[:, 384:640], AF.Copy, scale=ss[:, 5:6]),
                     reads=[dnres, ssres], writes=[cbres])
                pt, ptr = pT.next()
                for c in range(5):
                    S.op("pe", lambda e: e.transpose(pt[:, c, :], cb[:, c * 128:(c + 1) * 128], cm.identb[:]),
                         reads=[cbres, cm.idres], writes=[ptr])
                cT, cTres = cTr.next()
                S.op("dve", lambda e: e.tensor_copy(cT[:], pt[:, 0:5, :]), reads=[ptr], writes=[cTres])
                if DBG_PART < 2:
                    continue
                S.op("dve", lambda e: e.memset(ss[:, 8:16], 0.0), writes=[ssres])
                vb, vbres = vbr.next()
                for n in range(4):
                    pk, pkr = psW.next()
                    for c in range(2):
                        S.op("pe", lambda e: e.matmul(pk[:], cT[:, 3 + c, :], wukvb[:, c, n * 512:(n + 1) * 512],
                                                      start=(c == 0), stop=(c == 1)), reads=[cTres, wr], writes=[pkr])
                    pv = pk[:, :].rearrange("p (h w) -> p h w", h=2)
                    for hh in range(2 if DBG_PART != 21 else 0):
                        h = 2 * n + hh
                        S.op("act", lambda e: e.activation(junk2[:, 0:128], pv[:, hh, 0:128], AF.Square,
                                                           accum_out=ss[:, 8 + h:9 + h]), reads=[pkr], writes=[j2r, ssres])
                    if DBG_PART not in (21, 22):
                        for hh in range(2):
                            S.op("act", lambda e: e.activation(vb[:, 2 * n + hh, :], pv[:, hh, 128:256], AF.Copy),
                                 reads=[pkr], writes=[vbres])
                if DBG_PART in (21, 22, 23):
                    continue
                S.dma("pool", self.Vd[ti], vb[:, :, :].rearrange("p h d -> p (h d)"), reads=[vbres], writes=[self.Vres])
                S.op("dve", lambda e: e.tensor_scalar(ss[:, 16:24], ss[:, 8:16], ss[:, 2:3], None, ALU.add),
                     reads=[ssres], writes=[ssres])
                S.op("act", lambda e: e.activation(ss[:, 16:24], ss[:, 16:24], AF.Sqrt, scale=1.0 / QK, bias=EPS),
                     reads=[ssres], writes=[ssres])
                S.op("dve", lambda e: e.reciprocal(self.rstdk[:, ti, :], ss[:, 16:24]), reads=[ssres],
                     writes=[self.rstdkres])
                if DBG_PART < 3:
                    continue
                kTb, kTbres = kTbr.next()
                for g in range(2):
                    pk, pkr = psW.next()
                    for hh in range(4):
                        h = 4 * g + hh
                        for c in range(2):
                            S.op("pe", lambda e: e.matmul(pk[:, hh * 128:(hh + 1) * 128], wukvb[:, c, h * 256:h * 256 + 128],
                                                          cT[:, 3 + c, :], start=(c == 0), stop=(c == 1)),
                                 reads=[cTres, wr], writes=[pkr])
                    S.op("act", lambda e: e.activation(kTb[:, 4 * g:4 * g + 4, :],
                                                       pk[:, :].rearrange("p (h t) -> p h t", h=4), AF.Copy),
                         reads=[pkr], writes=[kTbres])
                S.dma("pool", self.KTd[:, :, ti * 128:(ti + 1) * 128].rearrange("h d t -> d h t"), kTb[:],
                      reads=[kTbres], writes=[self.KTres])
                if DBG_PART < 4:
                    continue
                kr, krres = krr.next()
                S.op("pool", lambda e: e.tensor_tensor(kr[:, 0, :], dn[:, 640:704], gkn[:, NOPE:QK], ALU.mult),
                     reads=[dnres, cr], writes=[krres])
                self.rope(S, "pool", kr[:, 0, :].rearrange("p (x q f) -> p x q f", x=2, q=2),
                          kr[:, 5, :].rearrange("p (x q f) -> p x q f", x=2, q=2),
                          [kr[:, 1 + i_, 0:32].rearrange("p (x f) -> p x f", x=2) for i_ in range(4)],
                          cosT[:, ti, :].rearrange("p (x f) -> p x f", x=2),
                          sinT[:, ti, :].rearrange("p (x f) -> p x f", x=2), krres, cr)
                krb, krbres = krbr.next()
                S.op("pool", lambda e: e.tensor_copy(krb[:], kr[:, 5, :]), reads=[krres], writes=[krbres])
                pt, ptr = pT.next()
                S.op("pe", lambda e: e.transpose(pt[0:ROPE, 0, :], krb[:], cm.identb[:]), reads=[krbres, cm.idres],
                     writes=[ptr])
                S.op("dve", lambda e: e.tensor_copy(self.KRT[:, ti * 128:(ti + 1) * 128], pt[0:ROPE, 0, :]),
                     reads=[ptr], writes=[self.KRTres])
                if not own or DBG_PART < 5:
                    continue
                qs, qsres = qsr.next()
                for n in range(3):
                    pq, pqr = psW.next()
                    for c in range(3):
                        S.op("pe", lambda e: e.matmul(pq[:], cT[:, c, :], wuqb[:, c, n * 512:(n + 1) * 512],
                                                      start=(c == 0), stop=(c == 2)), reads=[cTres, wr], writes=[pqr])
                    S.op("act" if n != 1 else "dve",
                         (lambda e: e.activation(qs[:, n * 512:(n + 1) * 512], pq[:], AF.Copy)) if n != 1
                         else (lambda e: e.tensor_copy(qs[:, n * 512:(n + 1) * 512], pq[:])),
                         reads=[pqr], writes=[qsres])
                qn, qnres = qnr.next()
                qv = qs[:, :].rearrange("p (h w) -> p h w", h=HEADS)
                S.op("pool", lambda e: e.tensor_tensor(qn[:], qv, qv, ALU.mult), reads=[qsres], writes=[qnres])
                S.op("dve", lambda e: e.tensor_reduce(ss[:, 24:32], qn[:], AX.X, ALU.add), reads=[qnres], writes=[ssres])
                S.op("act", lambda e: e.activation(ss[:, 24:32], ss[:, 24:32], AF.Sqrt, scale=1.0 / QK, bias=EPS),
                     reads=[ssres], writes=[ssres])
                S.op("dve", lambda e: e.reciprocal(ss[:, 24:32], ss[:, 24:32]), reads=[ssres], writes=[ssres])
                for h in range(HEADS):
                    S.op("dve",
                         lambda e: e.scalar_tensor_tensor(qn[:, h, :], qv[:, h, :], ss[:, 24 + h:25 + h], gq2[:],
                                                          ALU.mult, ALU.mult), reads=[qsres, ssres, cr], writes=[qnres])
                qb, qbres = qbr.next()
                S.op("act", lambda e: e.activation(qb[:, :, 0:NOPE], qn[:, :, 0:NOPE], AF.Copy), reads=[qnres], writes=[qbres])
                qt, qtres = qtr.next()
                tails = qn[:, :, NOPE:QK].rearrange("p h (x q f) -> p h x q f", x=2, q=2)
                otl = qb[:, :, NOPE:QK].rearrange("p h (x q f) -> p h x q f", x=2, q=2)
                bc = lambda a: a.rearrange("p (x f) -> p x f", x=2).unsqueeze(1).to_broadcast([128, HEADS, 2, 16])
                self.rope(S, "pool", tails, otl,
                          [qt[:, i_, :, :].rearrange("p h (x f) -> p h x f", x=2) for i_ in range(4)],
                          bc(cosT[:, ti, :]), bc(sinT[:, ti, :]), qnres, cr, ores=qbres, tres=qtres, five=True)
                ptn, ptnr = pT.next()
                for h in range(HEADS):
                    S.op("pe", lambda e: e.transpose(ptn[:, h, :], qb[:, h, 0:NOPE], cm.identb[:]),
                         reads=[qbres, cm.idres], writes=[ptnr])
                qTn, qTnres = qTnr.next()
                S.op("dve", lambda e: e.tensor_copy(qTn[:], ptn[:]), reads=[ptnr], writes=[qTnres])
                ptr_, ptrr = pT.next()
                for h in range(HEADS):
                    S.op("pe", lambda e: e.transpose(ptr_[0:ROPE, h, :], qb[:, h, NOPE:QK], cm.identb[:]),
                         reads=[qbres, cm.idres], writes=[ptrr])
                qTr, qTrres = qTrr.next()
                S.op("act", lambda e: e.activation(qTr[:], ptr_[0:ROPE, :, :], AF.Copy), reads=[ptrr], writes=[qTrres])
                S.dma("pool", self.QNd[:, :, ti * 128:(ti + 1) * 128].rearrange("h d t -> d h t"), qTn[:],
                      reads=[qTnres], writes=[self.QNres])
                S.dma("pool", self.QRd[:, :, ti * 128:(ti + 1) * 128].rearrange("h d t -> d h t"), qTr[:],
                      reads=[qTrres], writes=[self.QRres])
            S.barrier()

    @staticmethod
    def rope(S, eng, xin, xout, tmps, cos, sin, xres, cres, ores=None, tres=None, five=False):
        ores = ores or xres
        tres = tres or xres
        if five:
            x1, x2 = xin[:, :, :, 0, :], xin[:, :, :, 1, :]
            o1, o2 = xout[:, :, :, 0, :], xout[:, :, :, 1, :]
        else:
            x1, x2 = xin[:, :, 0, :], xin[:, :, 1, :]
            o1, o2 = xout[:, :, 0, :], xout[:, :, 1, :]
        t1, t2, t3, t4 = tmps
        S.op(eng, lambda e: e.tensor_tensor(t1, x1, cos, ALU.mult), reads=[xres, cres], writes=[tres])
        S.op(eng, lambda e: e.tensor_tensor(t2, x2, sin, ALU.mult), reads=[xres, cres], writes=[tres])
        S.op(eng, lambda e: e.tensor_tensor(t3, x1, sin, ALU.mult), reads=[xres, cres], writes=[tres])
        S.op(eng, lambda e: e.tensor_tensor(t4, x2, cos, ALU.mult), reads=[xres, cres], writes=[tres])
        S.op(eng, lambda e: e.tensor_tensor(o1, t1, t2, ALU.subtract), reads=[tres], writes=[ores])
        S.op(eng, lambda e: e.tensor_tensor(o2, t3, t4, ALU.add), reads=[tres], writes=[ores])

    def attention(self):
        C, S, cm = self.C, self.C.S, self.cm
        NK = NTILE * 128
        blocks = [(0, CTX, list(range(2)))] + [(CTX + 512 * i, 512, list(range(NTILE))) for i in range((NQ - CTX) // 512)]
        with ExitStack() as st:
            KTr = C.ring(st, 2, [128, NK], BF16)
            Vr = C.ring(st, 2, [128, NTILE, 128], BF16)
            QNr = C.ring(st, 2, [128, NQ], BF16)
            QRr = C.ring(st, 2, [ROPE, NQ], BF16)
            PTr = C.ring(st, 3, [128, 512], BF16)
            rlr = C.ring(st, 2, [128, 512])
            obr = C.ring(st, 2, [128, 512], BF16)
            psS = C.ring(st, 3, [128, 512], psum=True)
            psO = C.ring(st, 2, [128, 512], psum=True)
            psL = C.ring(st, 2, [128, 512], psum=True)
            for h in range(HEADS):
                KT, KTres = KTr.next()
                V, Vres = Vr.next()
                QN, QNres = QNr.next()
                QR, QRres = QRr.next()
                S.dma("sp", KT[:], self.KTd[h], reads=[self.KTres], writes=[KTres])
                for t0 in range(0, NTILE, 11):
                    S.dma("sp", V[:, t0:t0 + 11, :],
                          self.Vd[t0:t0 + 11, :, h * 128:(h + 1) * 128].rearrange("k p d -> p k d"),
                          reads=[self.Vres], writes=[Vres])
                S.dma("sp", QN[:], self.QNd[h], reads=[self.QNres], writes=[QNres])
                S.dma("sp", QR[:], self.QRd[h], reads=[self.QRres], writes=[QRres])
                for (q0, nq, keys) in blocks:
                    po, por = psO.next()
                    pl, plr = psL.next()
                    pend = {}

                    def emit_s(kt):
                        ps, psr = psS.next()
                        S.op("pe", lambda e: e.matmul(ps[:, 0:nq], KT[:, kt * 128:(kt + 1) * 128], QN[:, q0:q0 + nq],
                                                      start=True, stop=False), reads=[KTres, QNres], writes=[psr])
                        S.op("pe", lambda e: e.matmul(ps[:, 0:nq], self.KRT[:, kt * 128:(kt + 1) * 128], QR[:, q0:q0 + nq],
                                                      start=False, stop=True), reads=[self.KRTres, QRres], writes=[psr])
                        PT, PTres = PTr.next()
                        S.op("act", lambda e: e.activation(PT[:, 0:nq], ps[:, 0:nq], AF.Exp,
                                                           scale=self.rstdk[:, kt, h:h + 1]),
                             reads=[psr, self.rstdkres], writes=[PTres])
                        pend[kt] = (PT, PTres)

                    emit_s(keys[0])
                    for idx, kt in enumerate(keys):
                        if idx + 1 < len(keys):
                            emit_s(keys[idx + 1])
                        PT, PTres = pend.pop(kt)
                        first, last = idx == 0, idx == len(keys) - 1
                        S.op("pe", lambda e: e.matmul(po[:, 0:nq], V[:, kt, :], PT[:, 0:nq], start=first, stop=last),
                             reads=[Vres, PTres], writes=[por])
                        S.op("pe", lambda e: e.matmul(pl[:, 0:nq], cm.onesb[:], PT[:, 0:nq], start=first, stop=last),
                             reads=[cm.idres, PTres], writes=[plr])
                    rl, rlres = rlr.next()
                    ob, obres = obr.next()
                    S.op("dve", lambda e: e.reciprocal(rl[:, 0:nq], pl[:, 0:nq]), reads=[plr], writes=[rlres])
                    S.op("dve", lambda e: e.tensor_tensor(ob[:, 0:nq], po[:, 0:nq], rl[:, 0:nq], ALU.mult),
                         reads=[por, rlres], writes=[obres])
                    S.dma("pool", self.ATd[h, :, q0:q0 + nq], ob[:, 0:nq], reads=[obres], writes=[self.ATres])
            S.barrier()

    def out_proj(self, XMd, XMres):
        C, S, cm, nc = self.C, self.C.S, self.cm, self.nc
        wo = _inp(nc, "wo", [D, D])
        with ExitStack() as st:
            wob, wr = C.sb(st, [128, HEADS, D], BF16)
            stage = C.ring(st, 2, [128, D])
            for c in range(HEADS):
                stg, sr = stage.next()
                S.dma("sp", stg[:], wo[c * 128:(c + 1) * 128, :], writes=[sr])
                S.op("pool" if c % 2 else "dve", lambda e: e.tensor_copy(wob[:, c, :], stg[:]), reads=[sr], writes=[wr])
            aTr = C.ring(st, 2, [128, HEADS, 128], BF16)
            xr = C.ring(st, 2, [128, D])
            tr = C.ring(st, 2, [128, D])
            pso = C.ring(st, 4, [128, 512], psum=True)
            for ti in range(NOWN):
                src, j = self.src_rows(ti)
                aT, aTres = aTr.next()
                S.dma("sp", aT[:], self.ATd[:, :, ti * 128:(ti + 1) * 128].rearrange("h d t -> d h t"),
                      reads=[self.ATres], writes=[aTres])
                x, xres = xr.next()
                S.dma("sp", x[:], src, writes=[xres])
                gt, gr = cm.gabc[0][j]
                tt, ttr = tr.next()
                for nn in range(2):
                    po, por = pso.next()
                    for h in range(HEADS):
                        S.op("pe", lambda e: e.matmul(po[:], aT[:, h, :], wob[:, h, nn * 512:(nn + 1) * 512],
                                                      start=(h == 0), stop=(h == HEADS - 1)), reads=[aTres, wr], writes=[por])
                    S.op("dve", lambda e: e.tensor_tensor(tt[:, nn * 512:(nn + 1) * 512], po[:],
                                                          gt[:, nn * 512:(nn + 1) * 512], ALU.mult),
                         reads=[por, gr], writes=[ttr])
                S.op("pool", lambda e: e.tensor_tensor(tt[:], tt[:], x[:], ALU.add), reads=[ttr, xres], writes=[ttr])
                S.dma("pool", XMd[ti * 128:(ti + 1) * 128, :], tt[:], reads=[ttr], writes=[XMres])
            S.barrier()


def build_A(debug=False, stage=9):
    nc = bass.Bass("TRN2", target_bir_lowering=False)
    with ExitStack() as G:
        C = Ctx(nc, G)
        cm = Common(C, G, nc, 0)
        cm.modulation()
        mla = MLA(cm)
        mla.token_pass()
        if stage == 1:
            o1 = _out(nc, "kto", [HEADS, 128, NTILE * 128], BF16)
            o2 = _out(nc, "qno", [HEADS, 128, NQ], BF16)
            o3 = _out(nc, "qro", [HEADS, ROPE, NQ], BF16)
            o4 = _out(nc, "vo", [NTILE, 128, HEADS * 128], BF16)
            o5 = _out(nc, "kro", [ROPE, NTILE * 128], BF16)
            o6 = _out(nc, "rko", [128, NTILE * HEADS])
            rr = C.S.res()
            C.S.dma("sp", o1, mla.KTd, reads=[mla.KTres], writes=[rr])
            C.S.dma("sp", o2, mla.QNd, reads=[mla.QNres], writes=[rr])
            C.S.dma("sp", o3, mla.QRd, reads=[mla.QRres], writes=[rr])
            C.S.dma("sp", o4, mla.Vd, reads=[mla.Vres], writes=[rr])
            C.S.dma("sp", o5, mla.KRT[:], reads=[mla.KRTres], writes=[rr])
            C.S.dma("sp", o6, mla.rstdk[:, :, :].rearrange("p a b -> p (a b)"), reads=[mla.rstdkres], writes=[rr])
            C.S.finish([rr])
            return nc
        mla.attention()
        if stage == 2:
            o1 = _out(nc, "ato", [HEADS, 128, NQ], BF16)
            rr = C.S.res()
            C.S.dma("sp", o1, mla.ATd, reads=[mla.ATres], writes=[rr])
            C.S.finish([rr])
            return nc
        XMd, XMres = C.dram("XMd", [NQ, D])
        XTd, XTres = C.dram("XTd", [NQ, D])
        x1o = _out(nc, "x1o", [NQ, D])
        x1res = C.S.res()
        mla.out_proj(XMd, XMres)
        outs = [x1res]
        if debug:
            xmo = _out(nc, "xmo", [NQ, D])
            xmores = C.S.res()
            C.S.dma("sp", xmo, XMd, reads=[XMres], writes=[xmores])
            outs.append(xmores)
        tiles = [(ti * 128, 1 if ti < 2 else 0) for ti in range(NOWN)]
        cm.ffn_phase(tiles, XMd, XMres, XTd, XTres, x1o, x1res)
        C.S.finish(outs)
        print("ninst", C.S.ninst)
    return nc


def _colT(v):
    return np.ascontiguousarray(np.asarray(v, np.float32).reshape(-1, 128).T)


def _bc128(v):
    return np.ascontiguousarray(np.broadcast_to(np.asarray(v, np.float32)[None, :], (128, v.shape[-1])))


def rope_tables(half):
    inv = (10000.0 ** (-np.arange(16, dtype=np.float32) / 16)).astype(np.float32)
    i = np.arange(SEQ)
    s = i if half == 0 else SEQ - 1 - i
    row = (s // 64).astype(np.float32)
    col = (s % 64).astype(np.float32)
    ang = np.stack([row[:, None] * inv, col[:, None] * inv], 1).astype(np.float32)
    cos = np.cos(ang).reshape(SEQ // 128, 128, 32).transpose(1, 0, 2)
    sin = np.sin(ang).reshape(SEQ // 128, 128, 32).transpose(1, 0, 2)
    cosT = np.ones((128, NTILE, 32), np.float32)
    sinT = np.zeros((128, NTILE, 32), np.float32)
    cosT[:, 2:] = cos
    sinT[:, 2:] = sin
    return cosT, sinT


def inputs_A(inp, core):
    b, half = core // 2, core % 2
    l = 0
    x = np.asarray(inp["x"][b], np.float32)
    ctx = np.asarray(inp["ctx"][b], np.float32)
    if half == 1:
        x = x[::-1]
        ctx = ctx[::-1]
    c2 = np.stack([inp["c"][b], inp["c_ctx"]], 0).astype(np.float32)
    cosT, sinT = rope_tables(half)
    return dict(
        ident=np.eye(128, dtype=np.float32),
        cT=np.ascontiguousarray(c2.reshape(2, KC, 128).transpose(2, 1, 0)),
        ada_w=np.ascontiguousarray(inp["ada_w"][l]), ada_bT=_colT(inp["ada_b"][l]),
        gmix=_colT(inp["norm_mix"][l]), gffn=_colT(inp["norm_ffn"][l]),
        w1=np.ascontiguousarray(inp["ffn_w1"][l]), w3=np.ascontiguousarray(inp["ffn_w3"][l]),
        w2=np.ascontiguousarray(inp["ffn_w2"][l]),
        xf=np.ascontiguousarray(x), ctxl=np.ascontiguousarray(ctx),
        wdq=np.ascontiguousarray(inp["mla_w_dqkv"][0]), gql=_colT(inp["mla_g_q_lora"][0]),
        gkvl=_colT(inp["mla_g_kv_lora"][0]), wuq=np.ascontiguousarray(inp["mla_w_uq"][0]),
        wukv=np.ascontiguousarray(inp["mla_w_ukv"][0]), gqn=_bc128(inp["mla_g_qn"][0]), gkn=_bc128(inp["mla_g_kn"][0]),
        cosT=cosT, sinT=sinT, wo=np.ascontiguousarray(inp["mla_w_o"][0]),
    )


RH = 16
RN = 64
NLAT = SEQ // 2 // 128
DECAY_C = -math.exp(-0.5)


class RWKV:
    def __init__(self, cm, tag=""):
        self.cm = cm
        self.C, self.G, self.nc = cm.C, cm.G, cm.nc
        C, G = self.C, self.G
        self.NCOL = 1 + CTX + 2 + NLAT * 128 + 128
        self.HTd, self.HTres = C.dram("HTd" + tag, [128, KC, self.NCOL], BF16)
        self.H, self.Hres = C.sb(G, [128, 8, RN])

    def col0(self, ti):
        return 1 + ti * 128 if ti < 2 else 259 + (ti - 2) * 128

    def prep(self, xsrc, xsres, ntile_rows):
        C, S, cm = self.C, self.C.S, self.cm
        with ExitStack() as st:
            tmp = cm.norm_tmp(st, 2)
            xr = C.ring(st, 2, [128, D])
            hTr = C.ring(st, 2, [128, KC, 128], BF16)
            z, zr = C.sb(st, [128, KC, 2], BF16)
            S.op("dve", lambda e: e.memset(z[:], 0.0), writes=[zr])
            for c0 in (0, 257):
                S.dma("sp", self.HTd[:, :, c0:c0 + (1 if c0 == 0 else 2)], z[:, :, 0:(1 if c0 == 0 else 2)],
                      reads=[zr], writes=[self.HTres], allow_slow_non_contiguous=True)
            for ti in range(ntile_rows):
                j = 1 if ti < 2 else 0
                x, xres = xr.next()
                S.dma("sp", x[:], xsrc[ti * 128:(ti + 1) * 128, :], reads=[xsres], writes=[xres])
                hT, hTres = hTr.next()
                cm.norm_T(tmp, x[:], xres, cm.Am[:, j, :], cm.modcol[:, j, 0:8], hT, hTres, 0)
                c0 = self.col0(ti)
                n = 128 if ti < 2 + NLAT else 1
                S.dma("pool", self.HTd[:, :, c0:c0 + n], hT[:, :, 0:n], reads=[hTres], writes=[self.HTres],
                      allow_slow_non_contiguous=(n == 1))
            S.barrier()

    def load_weights(self, st, dirs, adirs, gate):
        C, S, nc = self.C, self.C.S, self.nc
        W = {}
        stage = C.ring(st, 2, [128, D])
        k = [0]

        def big(name):
            w = _inp(nc, name, [D, D])
            wb, wr = C.sb(st, [128, KC, D], BF16)
            for c in range(KC):
                stg, sr = stage.next()
                S.dma("sp" if k[0] % 2 == 0 else "pool", stg[:], w[c * 128:(c + 1) * 128, :], writes=[sr])
                S.op("pool" if k[0] % 2 == 0 else "dve", lambda e: e.tensor_copy(wb[:, c, :], stg[:]), reads=[sr], writes=[wr])
                k[0] += 1
            return wb, wr

        def small(name, rows, cols, dt=BF16):
            nch = (rows + 127) // 128
            w = _inp(nc, name, [rows, cols])
            wb, wr = C.sb(st, [min(rows, 128), nch, cols], dt)
            for c in range(nch):
                n = min(128, rows - c * 128)
                stg, sr = stage.next()
                S.dma("sp", stg[0:n, 0:cols], w[c * 128:c * 128 + n, :], writes=[sr])
                S.op("dve", lambda e: e.tensor_copy(wb[0:n, c, :], stg[0:n, 0:cols]), reads=[sr], writes=[wr])
            return wb, wr

        def const(name, shape):
            w = _inp(nc, name, shape)
            t, r = C.sb(st, shape)
            S.dma("sp", t[:], w, writes=[r])
            return t, r

        W["wr"] = big("rw_wr")
        W["wk"] = big("rw_wk")
        W["wv"] = big("rw_wv")
        W["mix"] = const("rw_mixT", [128, 6, KC])
        W["kk"] = const("rw_kk_bc", [128, D])
        W["ka"] = const("rw_ka_bc", [128, D])
        for dname in dirs:
            W["dw0" + dname] = const("rw_dw0_bc" + dname, [128, D])
            W["dw1" + dname] = small("rw_dw1" + dname, D, 64)
            W["dw2" + dname] = small("rw_dw2" + dname, 64, D)
        for dname in adirs:
            W["ia0" + dname] = const("rw_ia0_bc" + dname, [128, D])
            W["ia1" + dname] = small("rw_ia1" + dname, D, 64)
            W["ia2" + dname] = small("rw_ia2" + dname, 64, D)
        if gate:
            W["g1"] = small("rw_g1", D, 160)
            W["g2"] = small("rw_g2", 160, D)
        return W

    def features(self, tiles, wdir, adirs, F):
        C, S, cm = self.C, self.C.S, self.cm
        gate = "g" in F
        with ExitStack() as st:
            W = self.load_weights(st, [wdir], adirs, gate)
            hbr = C.ring(st, 2, [128, KC, 130], BF16)
            xx, xxr = C.sb(st, [128, KC, 128])
            tmpr = C.ring(st, 2, [128, KC, 128])
            xmr = C.ring(st, 2, [128, KC, 128], BF16)
            outr = C.ring(st, 4, [128, D])
            kt_, ktr = C.sb(st, [128, D])
            kkn, kknr = C.sb(st, [128, D])
            ar_ = C.ring(st, 2, [128, D])
            lor = C.ring(st, 2, [128, 2, 128], BF16)
            sm, smr = C.sb(st, [128, 64])
            ps = C.ring(st, 6, [128, 512], psum=True)
            ps1 = C.ring(st, 2, [128, 512], psum=True)
            mix, mixr = W["mix"]

            def store(name, t, tr, row0, eng="pool"):
                d, dr = F[name]
                S.dma(eng, d[row0:row0 + 128, :], t[:], reads=[tr], writes=[dr])

            def mixed(j, hb, hbres):
                tmp, tmpres = tmpr.next()
                xm, xmres = xmr.next()
                S.op("pool", lambda e: e.tensor_tensor(tmp[:], xx[:], mix[:, j, :].unsqueeze(2).to_broadcast([128, KC, 128]),
                                                       ALU.mult), reads=[xxr, mixr], writes=[tmpres])
                S.op("pool", lambda e: e.tensor_tensor(xm[:], tmp[:], hb[:, :, 1:129], ALU.add), reads=[tmpres, hbres],
                     writes=[xmres])
                return xm, xmres

            def proj(xm, xmres, wb, wr, dst, dres):
                for n in range(2):
                    p, pr = ps.next()
                    for c in range(KC):
                        S.op("pe", lambda e: e.matmul(p[:], xm[:, c, :], wb[:, c, n * 512:(n + 1) * 512],
                                                      start=(c == 0), stop=(c == KC - 1)), reads=[xmres, wr], writes=[pr])
                    S.op("act", lambda e: e.activation(dst[:, n * 512:(n + 1) * 512], p[:], AF.Copy), reads=[pr], writes=[dres])

            def lora(xm, xmres, w1, w2, bias, func1, func2, dst, dres, width=64):
                (w1b, w1r), (w2b, w2r) = w1, w2
                lo, lores = lor.next()
                nch = (width + 127) // 128
                for ch in range(nch):
                    m = min(128, width - ch * 128)
                    p1, p1r = ps1.next()
                    for c in range(KC):
                        S.op("pe", lambda e: e.matmul(p1[0:m, 0:128], w1b[:, c, ch * 128:ch * 128 + m], xm[:, c, :],
                                                      start=(c == 0), stop=(c == KC - 1)), reads=[xmres, w1r], writes=[p1r])
                    S.op("act", lambda e: e.activation(lo[0:m, ch, :], p1[0:m, 0:128], func1), reads=[p1r], writes=[lores])
                for n in range(2):
                    p, pr = ps.next()
                    for ch in range(nch):
                        m = min(128, width - ch * 128)
                        S.op("pe", lambda e: e.matmul(p[:], lo[0:m, ch, :], w2b[0:m, ch, n * 512:(n + 1) * 512],
                                                      start=(ch == 0), stop=(ch == nch - 1)), reads=[lores, w2r], writes=[pr])
                    if bias is not None:
                        S.op("dve", lambda e: e.tensor_tensor(dst[:, n * 512:(n + 1) * 512], p[:],
                                                              bias[0][:, n * 512:(n + 1) * 512], ALU.add),
                             reads=[pr, bias[1]], writes=[dres])
                    else:
                        S.op("act", lambda e: e.activation(dst[:, n * 512:(n + 1) * 512], p[:], AF.Copy), reads=[pr], writes=[dres])
                if func2 is not None:
                    S.op("act", lambda e: e.activation(dst[:], dst[:], func2), reads=[dres], writes=[dres])

            for (ti, is_lat) in tiles:
                row0 = ti * 128
                c0 = self.col0(ti)
                hb, hbres = hbr.next()
                S.dma("sp", hb[:], self.HTd[:, :, c0 - 1:c0 + 129], reads=[self.HTres], writes=[hbres])
                tmp, tmpres = tmpr.next()
                S.op("pool", lambda e: e.tensor_tensor(tmp[:], hb[:, :, 0:128], hb[:, :, 2:130], ALU.add), reads=[hbres],
                     writes=[tmpres])
                S.op("dve", lambda e: e.scalar_tensor_tensor(xx[:], tmp[:], 0.5, hb[:, :, 1:129], ALU.mult, ALU.subtract),
                     reads=[tmpres, hbres], writes=[xxr])
                if is_lat:
                    xm, xmres = mixed(0, hb, hbres)
                    o, ores = outr.next()
                    proj(xm, xmres, W["wr"][0], W["wr"][1], o, ores)
                    store("r", o, ores, row0)
                xm, xmres = mixed(3, hb, hbres)
                o, ores = outr.next()
                proj(xm, xmres, W["wv"][0], W["wv"][1], o, ores)
                store("v", o, ores, row0)
                xm, xmres = mixed(2, hb, hbres)
                proj(xm, xmres, W["wk"][0], W["wk"][1], kt_, ktr)
                o, ores = outr.next()
                S.op("pool", lambda e: e.tensor_tensor(kkn[:], kt_[:], W["kk"][0][:], ALU.mult), reads=[ktr, W["kk"][1]],
                     writes=[kknr])
                S.op("pool", lambda e: e.tensor_tensor(o[:], kkn[:], kkn[:], ALU.mult), reads=[kknr], writes=[ores])
                S.op("dve", lambda e: e.tensor_reduce(sm[:, 0:RH], o[:, :].rearrange("p (h n) -> p h n", h=RH), AX.X, ALU.add),
                     reads=[ores], writes=[smr])
                S.op("act", lambda e: e.activation(sm[:, 0:RH], sm[:, 0:RH], AF.Sqrt, bias=1e-12), reads=[smr], writes=[smr])
                S.op("dve", lambda e: e.reciprocal(sm[:, 0:RH], sm[:, 0:RH]), reads=[smr], writes=[smr])
                S.op("pool", lambda e: e.tensor_tensor(kkn[:, :].rearrange("p (h n) -> p h n", h=RH),
                                                       kkn[:, :].rearrange("p (h n) -> p h n", h=RH),
                                                       sm[:, 0:RH].unsqueeze(2).to_broadcast([128, RH, RN]), ALU.mult),
                     reads=[kknr, smr], writes=[kknr])
                store("kk", kkn, kknr, row0)
                xm, xmres = mixed(1, hb, hbres)
                o, ores = outr.next()
                lora(xm, xmres, W["dw1" + wdir], W["dw2" + wdir], W["dw0" + wdir], AF.Tanh, AF.Sigmoid, o, ores)
                S.op("pool", lambda e: e.tensor_scalar(o[:], o[:], DECAY_C, None, ALU.mult), reads=[ores], writes=[ores])
                store("lw", o, ores, row0)
                xm, xmres = mixed(4, hb, hbres)
                for dname in adirs:
                    if not is_lat and dname != wdir:
                        continue
                    a, ares = ar_.next()
                    lora(xm, xmres, W["ia1" + dname], W["ia2" + dname], W["ia0" + dname], AF.Copy, AF.Sigmoid, a, ares)
                    o, ores = outr.next()
                    S.op("dve", lambda e: e.scalar_tensor_tensor(o[:], a[:], -1.0, W["ka"][0][:], ALU.add, ALU.mult),
                         reads=[ares, W["ka"][1]], writes=[ores])
                    S.op("dve", lambda e: e.scalar_tensor_tensor(o[:], o[:], 1.0, kt_[:], ALU.add, ALU.mult),
                         reads=[ores, ktr], writes=[ores])
                    store("kd" + dname, o, ores, row0)
                    if dname == wdir:
                        o2, o2res = outr.next()
                        S.op("pool", lambda e: e.tensor_tensor(o2[:], kkn[:], a[:], ALU.mult), reads=[kknr, ares], writes=[o2res])
                        store("b", o2, o2res, row0)
                if gate and is_lat:
                    xm, xmres = mixed(5, hb, hbres)
                    o, ores = outr.next()
                    lora(xm, xmres, W["g1"], W["g2"], None, AF.Sigmoid, None, o, ores, width=160)
                    store("g", o, ores, row0)
            S.barrier()

    def scan(self, tiles, mode, F, kdname, Y, Yin=None):
        C, S, cm, nc = self.C, self.C.S, self.cm, self.nc
        tri_in = _inp(nc, "rw_tri%d" % mode, [128, 3, 128])
        msk_in = _inp(nc, "rw_msk%d" % mode, [128, 2, 512])
        with ExitStack() as st:
            tri, cr = C.sb(st, [128, 3, 128])
            msk, _ = C.sb(st, [128, 2, 512])
            S.dma("sp", tri[:], tri_in, writes=[cr])
            S.dma("sp", msk[:], msk_in, writes=[cr])
            tin = {k: C.sb(st, [128, D]) for k in ("r", "kd", "v", "kk", "b", "lw", "bh", "kh")}
            gamr = C.ring(st, 2, [128, D])
            ART, ARTr = C.sb(st, [128, 8, 2, 128])
            BT, BTr = C.sb(st, [128, 8, 128])
            KT, KTr = C.sb(st, [128, 8, 128])
            NA, NAr = C.sb(st, [128, RH, 2, 128])
            KA, KAr = C.sb(st, [128, RH, 2, 128])
            TT, TTr = C.sb(st, [128, RH, 128])
            Pr = C.ring(st, 3, [128, 4, 128])
            Qr = C.ring(st, 2, [128, 4, 128])
            Wsb, Wr = C.sb(st, [128, RH, RN])
            Usb, Ur = C.sb(st, [128, RH, RN])
            ysb, yr = C.sb(st, [128, D])
            ypv, ypr = C.sb(st, [128, D])
            gC, gCr = C.sb(st, [128, 8])
            Ht, Htr = C.sb(st, [128, 8, RN])
            ps = C.ring(st, 8, [128, 512], psum=True)
            H, Hres = self.H, self.Hres
            fl = lambda t: t[:, :, :].rearrange("p a b -> p (a b)")
            for (ti, need_out) in tiles:
                row0 = ti * 128
                names = ["kd", "v", "kk", "b", "lw"] + (["r"] if need_out else [])
                for i_, nm in enumerate(names):
                    src = F[kdname if nm == "kd" else nm]
                    S.dma("sp" if i_ % 2 == 0 else "pool", tin[nm][0][:], src[0][row0:row0 + 128, :], reads=[src[1]],
                          writes=[tin[nm][1]])
                (r, rres), (kd, kdres), (v, vres), (kk, kkres) = tin["r"], tin["kd"], tin["v"], tin["kk"]
                (b, bres), (lw, lwres), (bh, bhres), (kh, khres) = tin["b"], tin["lw"], tin["bh"], tin["kh"]
                pcs = {}
                for kind in (2, 0, 1):
                    for n in range(2):
                        p, pr = ps.next()
                        S.op("pe", lambda e: e.matmul(p[:], tri[:, kind, :], lw[:, n * 512:(n + 1) * 512], start=True, stop=True),
                             reads=[cr, lwres], writes=[pr])
                        pcs[(kind, n)] = (p, pr)
                pg, pgr = ps.next()
                for hp in range(8):
                    S.op("pe", lambda e: e.matmul(pg[:, hp:hp + 1], lw[:, hp * 128:(hp + 1) * 128], cm.onesf[:, 0:1],
                                                  start=True, stop=True), reads=[lwres, cm.idres], writes=[pgr])
                S.op("act", lambda e: e.activation(gC[:], pg[:, 0:8], AF.Exp), reads=[pgr], writes=[gCr])
                g, gr = gamr.next()
                for n in range(2):
                    p, pr = pcs[(2, n)]
                    S.op("act", lambda e: e.activation(g[:, n * 512:(n + 1) * 512], p[:], AF.Exp), reads=[pr], writes=[gr])
                S.op("pool", lambda e: e.tensor_tensor(bh[:], b[:], g[:], ALU.mult), reads=[bres, gr], writes=[bhres])
                S.op("dve", lambda e: e.tensor_tensor(kh[:], kd[:], g[:], ALU.mult), reads=[kdres, gr], writes=[khres])
                if need_out:
                    g, gr = gamr.next()
                    for n in range(2):
                        p, pr = pcs[(0, n)]
                        S.op("act", lambda e: e.activation(g[:, n * 512:(n + 1) * 512], p[:], AF.Exp), reads=[pr], writes=[gr])
                    S.op("pool", lambda e: e.tensor_tensor(r[:], r[:], g[:], ALU.mult), reads=[rres, gr], writes=[rres])
                g, gr = gamr.next()
                for n in range(2):
                    p, pr = pcs[(0, n)]
                    S.op("act", lambda e: e.activation(g[:, n * 512:(n + 1) * 512], p[:], AF.Exp, scale=-1.0), reads=[pr],
                         writes=[gr])
                S.op("pool", lambda e: e.tensor_tensor(b[:], b[:], g[:], ALU.mult), reads=[bres, gr], writes=[bres])
                S.op("dve", lambda e: e.tensor_tensor(kd[:], kd[:], g[:], ALU.mult), reads=[kdres, gr], writes=[kdres])
                g, gr = gamr.next()
                for n in range(2):
                    p, pr = pcs[(1, n)]
                    S.op("act", lambda e: e.activation(g[:, n * 512:(n + 1) * 512], p[:], AF.Exp), reads=[pr], writes=[gr])
                S.op("dve", lambda e: e.scalar_tensor_tensor(kk[:], kk[:], -1.0, g[:], ALU.mult, ALU.mult),
                     reads=[kkres, gr], writes=[kkres])
                for hp0 in range(0, 8, 2):
                    p, pr = ps.next()
                    for i_ in range(2):
                        hp = hp0 + i_
                        S.op("pe", lambda e: e.transpose(p[:, (2 * i_) * 128:(2 * i_ + 1) * 128], kk[:, hp * 128:(hp + 1) * 128],
                                                         cm.identf[:]), reads=[kkres, cm.idres], writes=[pr])
                        if need_out:
                            S.op("pe", lambda e: e.transpose(p[:, (2 * i_ + 1) * 128:(2 * i_ + 2) * 128],
                                                             r[:, hp * 128:(hp + 1) * 128], cm.identf[:]),
                                 reads=[rres, cm.idres], writes=[pr])
                    S.op("act", lambda e: e.activation(ART[:, hp0:hp0 + 2, :, :].rearrange("p a b c -> p (a b c)"), p[:], AF.Copy),
                         reads=[pr], writes=[ARTr])
                for (src, sres, dst, dres) in ((b, bres, BT, BTr), (kd, kdres, KT, KTr)):
                    for hp0 in range(0, 8, 4):
                        p, pr = ps.next()
                        for i_ in range(4):
                            hp = hp0 + i_
                            S.op("pe", lambda e: e.transpose(p[:, i_ * 128:(i_ + 1) * 128], src[:, hp * 128:(hp + 1) * 128],
                                                             cm.identf[:]), reads=[sres, cm.idres], writes=[pr])
                        S.op("dve", lambda e: e.tensor_copy(dst[:, hp0:hp0 + 4, :].rearrange("p a b -> p (a b)"), p[:]),
                             reads=[pr], writes=[dres])
                for g4 in range(4):
                    pm1 = [ps.next() for _ in range(2)]
                    pm2 = [ps.next() for _ in range(2)]
                    pm3 = ps.next()
                    for i_ in range(4):
                        h = 4 * g4 + i_
                        hp, h2 = h // 2, h % 2
                        P_ = slice(h2 * 64, h2 * 64 + 64)
                        ar2 = ART[P_, hp, :, :].rearrange("p a b -> p (a b)")
                        p, pr = pm1[i_ // 2]
                        S.op("pe", lambda e: e.matmul(p[:, (i_ % 2) * 256:(i_ % 2) * 256 + 256], BT[P_, hp, :], ar2,
                                                      start=True, stop=True), reads=[BTr, ARTr], writes=[pr])
                        p, pr = pm2[i_ // 2]
                        S.op("pe", lambda e: e.matmul(p[:, (i_ % 2) * 256:(i_ % 2) * 256 + 256], KT[P_, hp, :], ar2,
                                                      start=True, stop=True), reads=[KTr, ARTr], writes=[pr])
                        p, pr = pm3
                        S.op("pe", lambda e: e.matmul(p[:, i_ * 128:(i_ + 1) * 128], ART[P_, hp, 0, :], BT[P_, hp, :],
                                                      start=True, stop=True), reads=[BTr, ARTr], writes=[pr])
                    for b2 in range(2):
                        h0 = 4 * g4 + 2 * b2
                        S.op("dve", lambda e: e.tensor_tensor(NA[:, h0:h0 + 2, :, :].rearrange("p a b c -> p (a b c)"),
                                                              pm1[b2][0][:], msk[:, 0, :], ALU.mult),
                             reads=[pm1[b2][1], cr], writes=[NAr])
                        S.op("dve", lambda e: e.tensor_tensor(KA[:, h0:h0 + 2, :, :].rearrange("p a b c -> p (a b c)"),
                                                              pm2[b2][0][:], msk[:, 0, :], ALU.mult),
                             reads=[pm2[b2][1], cr], writes=[KAr])
                    Pc, Pcr = Pr.next()
                    S.op("dve", lambda e: e.tensor_tensor(Pc[:, :, :].rearrange("p a b -> p (a b)"), pm3[0][:], msk[:, 1, :],
                                                          ALU.mult), reads=[pm3[1], cr], writes=[Pcr])
                    hs = slice(4 * g4, 4 * g4 + 4)
                    S.op("pool", lambda e: e.tensor_tensor(TT[:, hs, :], NA[:, hs, 0, :],
                                                           cm.identf[:, :].unsqueeze(1).to_broadcast([128, 4, 128]), ALU.add),
                         reads=[NAr, cm.idres], writes=[TTr])
                    Qc, Qcr = None, NAr
                    for j in range(6):
                        pP, pPr = ps.next()
                        for i_ in range(4):
                            q_i = NA[:, 4 * g4 + i_, 0, :] if Qc is None else Qc[:, i_, :]
                            S.op("pe", lambda e: e.matmul(pP[:, i_ * 128:(i_ + 1) * 128], q_i, Pc[:, i_, :], start=True, stop=True),
                                 reads=[Qcr, Pcr], writes=[pPr])
                        if j < 5:
                            pQ, pQr = ps.next()
                            for i_ in range(4):
                                q_i = NA[:, 4 * g4 + i_, 0, :] if Qc is None else Qc[:, i_, :]
                                S.op("pe", lambda e: e.matmul(pQ[:, i_ * 128:(i_ + 1) * 128], Pc[:, i_, :], q_i, start=True, stop=True),
                                     reads=[Qcr, Pcr], writes=[pQr])
                        Pn, Pnr = Pr.next()
                        S.op("act", lambda e: e.activation(Pn[:, :, :].rearrange("p a b -> p (a b)"), pP[:], AF.Copy),
                             reads=[pPr], writes=[Pnr])
                        if j < 5:
                            Qn, Qnr = Qr.next()
                            S.op("dve", lambda e: e.tensor_copy(Qn[:, :, :].rearrange("p a b -> p (a b)"), pQ[:]),
                                 reads=[pQr], writes=[Qnr])
                        pT_, pTr = ps.next()
                        for i_ in range(4):
                            S.op("pe", lambda e: e.matmul(pT_[:, i_ * 128:(i_ + 1) * 128], Pn[:, i_, :], TT[:, 4 * g4 + i_, :],
                                                          start=True, stop=True), reads=[Pnr, TTr], writes=[pTr])
                        S.op("dve", lambda e: e.tensor_tensor(TT[:, hs, :].rearrange("p a b -> p (a b)"), pT_[:],
                                                              TT[:, hs, :].rearrange("p a b -> p (a b)"), ALU.add),
                             reads=[pTr, TTr], writes=[TTr])
                        Pc, Pcr = Pn, Pnr
                        if j < 5:
                            Qc, Qcr = Qn, Qnr
                hv = lambda t, h: t[:, h * RN:(h + 1) * RN]
                pw = [ps.next() for _ in range(2)]
                for h in range(RH):
                    hp, h2 = h // 2, h % 2
                    P_ = slice(h2 * 64, h2 * 64 + 64)
                    p, pr = pw[h // 8]
                    o = p[:, (h % 8) * RN:(h % 8 + 1) * RN]
                    S.op("pe", lambda e: e.matmul(o, ART[P_, hp, 0, :], H[P_, hp, :], start=True, stop=False),
                         reads=[ARTr, Hres], writes=[pr])
                    S.op("pe", lambda e: e.matmul(o, KA[:, h, 0, :], hv(v, h), start=False, stop=True),
                         reads=[KAr, vres], writes=[pr])
                for b2 in range(2):
                    S.op("act", lambda e: e.activation(Wsb[:, 8 * b2:8 * b2 + 8, :].rearrange("p a b -> p (a b)"), pw[b2][0][:],
                                                       AF.Copy), reads=[pw[b2][1]], writes=[Wr])
                pu = [ps.next() for _ in range(2)]
                for h in range(RH):
                    p, pr = pu[h // 8]
                    S.op("pe", lambda e: e.matmul(p[:, (h % 8) * RN:(h % 8 + 1) * RN], TT[:, h, :], Wsb[:, h, :],
                                                  start=True, stop=True), reads=[TTr, Wr], writes=[pr])
                for b2 in range(2):
                    S.op("dve", lambda e: e.tensor_copy(Usb[:, 8 * b2:8 * b2 + 8, :].rearrange("p a b -> p (a b)"), pu[b2][0][:]),
                         reads=[pu[b2][1]], writes=[Ur])
                if need_out:
                    yrow = (ti - 2) * 128
                    yadd = Yin is not None
                    if yadd:
                        S.dma("sp", ypv[:], Yin[0][yrow:yrow + 128, :], reads=[Yin[1]], writes=[ypr])
                    py = [ps.next() for _ in range(2)]
                    for h in range(RH):
                        hp, h2 = h // 2, h % 2
                        P_ = slice(h2 * 64, h2 * 64 + 64)
                        p, pr = py[h // 8]
                        o = p[:, (h % 8) * RN:(h % 8 + 1) * RN]
                        S.op("pe", lambda e: e.matmul(o, ART[P_, hp, 1, :], H[P_, hp, :], start=True, stop=False),
                             reads=[ARTr, Hres], writes=[pr])
                        S.op("pe", lambda e: e.matmul(o, NA[:, h, 1, :], Usb[:, h, :], start=False, stop=False),
                             reads=[NAr, Ur], writes=[pr])
                        S.op("pe", lambda e: e.matmul(o, KA[:, h, 1, :], hv(v, h), start=False, stop=True),
                             reads=[KAr, vres], writes=[pr])
                    for b2 in range(2):
                        if yadd:
                            S.op("dve", lambda e: e.tensor_tensor(ysb[:, 512 * b2:512 * b2 + 512], py[b2][0][:],
                                                                  ypv[:, 512 * b2:512 * b2 + 512], ALU.add),
                                 reads=[py[b2][1], ypr], writes=[yr])
                        else:
                            S.op("act", lambda e: e.activation(ysb[:, 512 * b2:512 * b2 + 512], py[b2][0][:], AF.Copy),
                                 reads=[py[b2][1]], writes=[yr])
                    S.dma("pool", Y[0][yrow:yrow + 128, :], ysb[:], reads=[yr], writes=[Y[1]])
                ph, phr = ps.next()
                for h in range(RH):
                    hp, h2 = h // 2, h % 2
                    o = ph[h2 * 64:h2 * 64 + 64, hp * RN:(hp + 1) * RN]
                    S.op("pe", lambda e: e.matmul(o, hv(bh, h), Usb[:, h, :], start=True, stop=False),
                         reads=[bhres, Ur], writes=[phr])
                    S.op("pe", lambda e: e.matmul(o, hv(kh, h), hv(v, h), start=False, stop=True),
                         reads=[khres, vres], writes=[phr])
                S.op("dve", lambda e: e.tensor_tensor(Ht[:], H[:], gC[:, :].unsqueeze(2).to_broadcast([128, 8, RN]), ALU.mult),
                     reads=[Hres, gCr], writes=[Htr])
                S.op("dve", lambda e: e.tensor_tensor(fl(H), ph[:, 0:512], fl(Ht), ALU.add), reads=[phr, Htr], writes=[Hres])
            S.barrier()


NTOK = NOWN * 128


def rw_scratch(C, names):
    return {n: C.dram("F_" + n, [NTOK, D]) for n in names}


def build_B():
    nc = bass.Bass("TRN2", target_bir_lowering=False)
    with ExitStack() as G:
        C = Ctx(nc, G)
        cm = Common(C, G, nc, 1)
        cm.modulation()
        rw = RWKV(cm)
        xs = _inp(nc, "xs", [(NOWN + 1) * 128, D])
        xsres = C.S.res()
        rw.prep(xs, xsres, NOWN + 1)
        F = rw_scratch(C, ["r", "v", "kk", "lw", "kdA", "b"])
        tiles = [(ti, ti >= 2) for ti in range(NOWN)]
        rw.features(tiles, "A", ["A"], F)
        yA = _out(nc, "yA", [NLAT * 128, D])
        yres = C.S.res()
        C.S.op("dve", lambda e: e.memset(rw.H[:], 0.0), writes=[rw.Hres])
        rw.scan(tiles, 0, F, "kdA", (yA, yres))
        Ho = _out(nc, "Hout", [128, 8 * RN])
        hores = C.S.res()
        C.S.dma("sp", Ho, rw.H[:, :, :].rearrange("p a b -> p (a b)"), reads=[rw.Hres], writes=[hores])
        C.S.finish([yres, hores])
        print("ninst", C.S.ninst)
    return nc


def scan_consts(mode):
    i = np.arange(128)
    s, t = i[:, None], i[None, :]
    early = (s < t) if mode == 0 else (s > t)
    incl = early | (s == t)
    later = ~incl
    tri = np.stack([incl, early, later], 1).astype(np.float32)
    su, iu = early.astype(np.float32), incl.astype(np.float32)
    sl = early.T.astype(np.float32)
    msk = np.stack([np.concatenate([su, iu, su, iu], 1), np.concatenate([sl] * 4, 1)], 1)
    return np.ascontiguousarray(tri), np.ascontiguousarray(msk)


def inputs_common(inp, core, l):
    b = core // 2
    c2 = np.stack([inp["c"][b], inp["c_ctx"]], 0).astype(np.float32)
    return dict(
        ident=np.eye(128, dtype=np.float32),
        cT=np.ascontiguousarray(c2.reshape(2, KC, 128).transpose(2, 1, 0)),
        ada_w=np.ascontiguousarray(inp["ada_w"][l]), ada_bT=_colT(inp["ada_b"][l]),
        gmix=_colT(inp["norm_mix"][l]), gffn=_colT(inp["norm_ffn"][l]))


def inputs_rw(inp, core, dirs, adirs, modes, gate):
    half = core % 2
    dmap = {"A": half, "B": 1 - half}
    f = lambda a: np.ascontiguousarray(np.asarray(a, np.float32))
    out = dict(
        rw_wr=f(inp["rw_w_r"][0]), rw_wk=f(inp["rw_w_k"][0]), rw_wv=f(inp["rw_w_v"][0]),
        rw_mixT=f(np.asarray(inp["rw_mix"][0]).reshape(6, KC, 128).transpose(2, 0, 1)),
        rw_kk_bc=_bc128(inp["rw_k_k"][0]), rw_ka_bc=_bc128(inp["rw_k_a"][0]))
    for dn in dirs:
        d = dmap[dn]
        out["rw_dw0_bc" + dn] = _bc128(inp["rw_decay_w0"][0, d])
        out["rw_dw1" + dn] = f(inp["rw_decay_w1"][0, d])
        out["rw_dw2" + dn] = f(inp["rw_decay_w2"][0, d])
    for dn in adirs:
        d = dmap[dn]
        out["rw_ia0_bc" + dn] = _bc128(inp["rw_iclr_a0"][0, d])
        out["rw_ia1" + dn] = f(inp["rw_iclr_a1"][0, d])
        out["rw_ia2" + dn] = f(inp["rw_iclr_a2"][0, d])
    for m in modes:
        out["rw_tri%d" % m], out["rw_msk%d" % m] = scan_consts(m)
    if gate:
        out["rw_g1"] = f(inp["rw_gate_g1"][0])
        out["rw_g2"] = f(inp["rw_gate_g2"][0])
    return out


def rw_output(rw, F, Y, xs, xsres, XMd, XMres):
    C, S, cm, nc = rw.C, rw.C.S, rw.cm, rw.nc
    wo = _inp(nc, "rw_wo", [D, D])
    gnw_in = _inp(nc, "rw_gnw_bc", [128, D])
    gnb_in = _inp(nc, "rw_gnb_bc", [128, D])
    rk_in = _inp(nc, "rw_rk_bc", [128, D])
    with ExitStack() as st:
        wob, wr = C.sb(st, [128, KC, D], BF16)
        stage = C.ring(st, 2, [128, D])
        for c in range(KC):
            stg, sr = stage.next()
            S.dma("sp", stg[:], wo[c * 128:(c + 1) * 128, :], writes=[sr])
            S.op("pool" if c % 2 else "dve", lambda e: e.tensor_copy(wob[:, c, :], stg[:]), reads=[sr], writes=[wr])
        gnw, cr = C.sb(st, [128, D])
        gnb, _ = C.sb(st, [128, D])
        rk, _ = C.sb(st, [128, D])
        for t_, a_ in ((gnw, gnw_in), (gnb, gnb_in), (rk, rk_in)):
            S.dma("sp", t_[:], a_, writes=[cr])
        tin = {k: C.sb(st, [128, D]) for k in ("y", "r", "v", "kdA", "kdB", "g", "x")}
        t1, t1r = C.sb(st, [128, D])
        t2, t2r = C.sb(st, [128, D])
        ob, obr = C.sb(st, [128, D], BF16)
        oTr = C.ring(st, 2, [128, KC, 128], BF16)
        sm, smr = C.sb(st, [128, 64])
        tr = C.ring(st, 2, [128, D])
        pst = C.ring(st, 2, [128, KC, 128], BF16, psum=True)
        pso = C.ring(st, 4, [128, 512], psum=True)
        h3 = lambda t: t[:, :].rearrange("p (h n) -> p h n", h=RH)
        bc = lambda a: a.unsqueeze(2).to_broadcast([128, RH, RN])
        for ti in range(2, 2 + NLAT):
            row0 = ti * 128
            yrow = (ti - 2) * 128
            for i_, nm in enumerate(("y", "r", "v", "kdA", "kdB", "g", "x")):
                src = {"y": (Y[0], Y[1], yrow), "x": (xs, xsres, row0)}.get(nm, None)
                if src is None:
                    src = (F[nm][0], F[nm][1], row0)
                S.dma("sp" if i_ % 2 == 0 else "pool", tin[nm][0][:], src[0][src[2]:src[2] + 128, :], reads=[src[1]],
                      writes=[tin[nm][1]])
            (y, yr), (r, rr), (v, vr), (ka, kar), (kb, kbr), (g, gr), (x, xr) = [tin[k] for k in
                                                                                 ("y", "r", "v", "kdA", "kdB", "g", "x")]
            S.op("dve", lambda e: e.tensor_reduce(sm[:, 0:16], h3(y), AX.X, ALU.add), reads=[yr], writes=[smr])
            S.op("dve", lambda e: e.tensor_scalar(sm[:, 0:16], sm[:, 0:16], -1.0 / RN, None, ALU.mult), reads=[smr], writes=[smr])
            S.op("pool", lambda e: e.tensor_tensor(h3(t1), h3(y), bc(sm[:, 0:16]), ALU.add), reads=[yr, smr], writes=[t1r])
            S.op("pool", lambda e: e.tensor_tensor(t2[:], t1[:], t1[:], ALU.mult), reads=[t1r], writes=[t2r])
            S.op("dve", lambda e: e.tensor_reduce(sm[:, 16:32], h3(t2), AX.X, ALU.add), reads=[t2r], writes=[smr])
            S.op("act", lambda e: e.activation(sm[:, 16:32], sm[:, 16:32], AF.Sqrt, scale=1.0 / RN, bias=64e-5),
                 reads=[smr], writes=[smr])
            S.op("dve", lambda e: e.reciprocal(sm[:, 16:32], sm[:, 16:32]), reads=[smr], writes=[smr])
            S.op("pool", lambda e: e.tensor_tensor(h3(t1), h3(t1), bc(sm[:, 16:32]), ALU.mult), reads=[t1r, smr], writes=[t1r])
            S.op("pool", lambda e: e.tensor_tensor(t1[:], t1[:], gnw[:], ALU.mult), reads=[t1r, cr], writes=[t1r])
            S.op("pool", lambda e: e.tensor_tensor(t1[:], t1[:], gnb[:], ALU.add), reads=[t1r, cr], writes=[t1r])
            S.op("dve", lambda e: e.tensor_tensor(t2[:], ka[:], kb[:], ALU.add), reads=[kar, kbr], writes=[t2r])
            S.op("dve", lambda e: e.tensor_tensor(t2[:], t2[:], r[:], ALU.mult), reads=[t2r, rr], writes=[t2r])
            S.op("dve", lambda e: e.tensor_tensor(t2[:], t2[:], rk[:], ALU.mult), reads=[t2r, cr], writes=[t2r])
            S.op("dve", lambda e: e.tensor_reduce(sm[:, 32:48], h3(t2), AX.X, ALU.add), reads=[t2r], writes=[smr])
            S.op("pool", lambda e: e.tensor_tensor(h3(t2), h3(v), bc(sm[:, 32:48]), ALU.mult), reads=[vr, smr], writes=[t2r])
            S.op("pool", lambda e: e.tensor_tensor(t1[:], t1[:], t2[:], ALU.add), reads=[t1r, t2r], writes=[t1r])
            S.op("dve", lambda e: e.tensor_tensor(ob[:], t1[:], g[:], ALU.mult), reads=[t1r, gr], writes=[obr])
            pt, ptr = pst.next()
            for c in range(KC):
                S.op("pe", lambda e: e.transpose(pt[:, c, :], ob[:, c * 128:(c + 1) * 128], cm.identb[:]),
                     reads=[obr, cm.idres], writes=[ptr])
            oT, oTres = oTr.next()
            S.op("act", lambda e: e.activation(oT[:], pt[:], AF.Copy), reads=[ptr], writes=[oTres])
            gt, gtr = cm.gabc[0][0]
            tt, ttr = tr.next()
            for nn in range(2):
                po, por = pso.next()
                for c in range(KC):
                    S.op("pe", lambda e: e.matmul(po[:], oT[:, c, :], wob[:, c, nn * 512:(nn + 1) * 512],
                                                  start=(c == 0), stop=(c == KC - 1)), reads=[oTres, wr], writes=[por])
                S.op("dve", lambda e: e.tensor_tensor(tt[:, nn * 512:(nn + 1) * 512], po[:], gt[:, nn * 512:(nn + 1) * 512],
                                                      ALU.mult), reads=[por, gtr], writes=[ttr])
            S.op("pool", lambda e: e.tensor_tensor(tt[:], tt[:], x[:], ALU.add), reads=[ttr, xr], writes=[ttr])
            S.dma("pool", XMd[yrow:yrow + 128, :], tt[:], reads=[ttr], writes=[XMres])
        S.barrier()


def build_C(debug=False):
    nc = bass.Bass("TRN2", target_bir_lowering=False)
    with ExitStack() as G:
        C = Ctx(nc, G)
        cm = Common(C, G, nc, 1)
        cm.modulation()
        rw = RWKV(cm)
        xs = _inp(nc, "xs", [(NOWN + 1) * 128, D])
        xsres = C.S.res()
        rw.prep(xs, xsres, NOWN + 1)
        F = rw_scratch(C, ["r", "v", "kk", "lw", "kdA", "kdB", "b", "g"])
        lat = [(ti, True) for ti in range(2, 2 + NLAT)]
        rw.features(lat, "B", ["B", "A"], F)
        yA = _inp(nc, "yA", [NLAT * 128, D])
        yAres = C.S.res()
        H0 = _inp(nc, "H0", [128, 8 * RN])
        Y = C.dram("Ytot", [NLAT * 128, D])
        C.S.dma("sp", rw.H[:, :, :].rearrange("p a b -> p (a b)"), H0, writes=[rw.Hres])
        rw.scan(lat[::-1], 1, F, "kdB", Y, Yin=(yA, yAres))
        XMd, XMres = C.dram("XMd", [NLAT * 128, D])
        XTd, XTres = C.dram("XTd", [NLAT * 128, D])
        rw_output(rw, F, Y, xs, xsres, XMd, XMres)
        xo = _out(nc, "xo", [NLAT * 128, D])
        xores = C.S.res()
        outs = [xores]
        if debug:
            for nm, (src, sres) in (("yo", Y), ("xmo", (XMd, XMres))):
                o_ = _out(nc, nm, [NLAT * 128, D])
                r_ = C.S.res()
                C.S.dma("sp", o_, src, reads=[sres], writes=[r_])
                outs.append(r_)
        tiles = [(i * 128, 0) for i in range(NLAT)]
        cm.ffn_phase(tiles, XMd, XMres, XTd, XTres, xo, xores)
        C.S.finish(outs)
        print("ninst", C.S.ninst)
    return nc


def inputs_C_extra(inp):
    f = lambda a: np.ascontiguousarray(np.asarray(a, np.float32))
    return dict(rw_wo=f(inp["rw_w_o"][0]), rw_gnw_bc=_bc128(inp["rw_gn_w"][0]), rw_gnb_bc=_bc128(inp["rw_gn_b"][0]),
                rw_rk_bc=_bc128(np.asarray(inp["rw_r_k"][0]).reshape(-1)),
                w1=f(inp["ffn_w1"][1]), w3=f(inp["ffn_w3"][1]), w2=f(inp["ffn_w2"][1]))


def kernel(**inp):
    inp = {k: np.asarray(v) for k, v in inp.items()}
    cores = list(range(NCORE))
    ncA = build_A()
    resA = run_bass_kernel_spmd(ncA, [inputs_A(inp, c) for c in cores], core_ids=cores).results
    x1o = [np.asarray(resA[c]["x1o"], np.float32) for c in cores]
    def xs_of(c):
        halo = np.zeros((128, D), np.float32)
        halo[0] = x1o[c ^ 1][NQ - 1]
        return np.ascontiguousarray(np.concatenate([x1o[c], halo], 0))
    xs = [xs_of(c) for c in cores]
    ncB = build_B()
    insB = []
    for c in cores:
        d = dict(inputs_common(inp, c, 1))
        d.update(inputs_rw(inp, c, ["A"], ["A"], [0], False))
        d["xs"] = xs[c]
        insB.append(d)
    resB = run_bass_kernel_spmd(ncB, insB, core_ids=cores).results
    ncC = build_C()
    insC = []
    for c in cores:
        d = dict(inputs_common(inp, c, 1))
        d.update(inputs_rw(inp, c, ["B"], ["B", "A"], [1], True))
        d.update(inputs_C_extra(inp))
        d["xs"] = xs[c]
        d["H0"] = np.ascontiguousarray(np.asarray(resB[c ^ 1]["Hout"], np.float32))
        d["yA"] = np.ascontiguousarray(np.asarray(resB[c]["yA"], np.float32))
        insC.append(d)
    resC = run_bass_kernel_spmd(ncC, insC, core_ids=cores).results
    out = np.empty((4, SEQ, D), np.float32)
    for c in cores:
        b, half = c // 2, c % 2
        xo = np.asarray(resC[c]["xo"], np.float32)
        if half == 0:
            out[b, :SEQ // 2] = xo
        else:
            out[b, SEQ // 2:] = xo[::-1]
    return out
```
